# Optimizing a Trainium2 kernel written in Bass

```python
import math
import jax, jax.numpy as jnp
from jax import lax
import numpy as np

D_MODEL = 2048
BATCH = 2
SEQ = 4096
DEPTH = 4

HEAD_DIM = 128
BLOCK = 128
GMLP_WIDTH = D_MODEL // 2
GMLP_GROUPS = 4
GMLP_GROUP_CH = GMLP_WIDTH // GMLP_GROUPS
FOX_HEADS = D_MODEL // 256
FOX_WIDTH = FOX_HEADS * HEAD_DIM
DIFF_HEADS = D_MODEL // 512
DIFF_QK_WIDTH = DIFF_HEADS * 2 * HEAD_DIM
DIFF_V_DIM = 2 * HEAD_DIM
DIFF_WIDTH = DIFF_HEADS * DIFF_V_DIM
N_BRANCH = 3
D_FF = 4 * D_MODEL
ROPE_THETA = 10000.0
LN_EPS = 1e-5
RMS_EPS = 1e-5
NEG_INF = -1e30
DEEPNORM_ALPHA = (2 * DEPTH) ** 0.25
DEEPNORM_BETA = (8 * DEPTH) ** -0.25

OFF_U = 0
OFF_V = OFF_U + GMLP_WIDTH
OFF_FQ = OFF_V + GMLP_WIDTH
OFF_FK = OFF_FQ + FOX_WIDTH
OFF_FV = OFF_FK + FOX_WIDTH
OFF_FF = OFF_FV + FOX_WIDTH
OFF_DQ = OFF_FF + FOX_HEADS
OFF_DK = OFF_DQ + DIFF_QK_WIDTH
OFF_DV = OFF_DK + DIFF_QK_WIDTH
IN_WIDTH = OFF_DV + DIFF_WIDTH

kernel_name = "hybrid_gated_gmlp_fox_diffattn_deepnorm"


def layer_norm(x, g, b):
    xf = x.astype(jnp.float32)
    mu = jnp.mean(xf, axis=-1, keepdims=True)
    var = jnp.mean(jnp.square(xf - mu), axis=-1, keepdims=True)
    y = (xf - mu) * lax.rsqrt(var + LN_EPS)
    return (y * g.astype(jnp.float32) + b.astype(jnp.float32)).astype(x.dtype)


def rms_norm(x, g):
    xf = x.astype(jnp.float32)
    y = xf * lax.rsqrt(jnp.mean(jnp.square(xf), axis=-1, keepdims=True) + RMS_EPS)
    return y * g.astype(jnp.float32)


def apply_rope(x, cos, sin):
    half = x.shape[-1] // 2
    x1, x2 = x[..., :half], x[..., half:]
    return jnp.concatenate([x1 * cos - x2 * sin, x2 * cos + x1 * sin], axis=-1)


def gmlp_spatial_gating(u, v, ln_g, ln_b, w_s, b_s):
    B, S, _ = v.shape
    v = layer_norm(v, ln_g, ln_b)
    vb = v.reshape(B, S // BLOCK, BLOCK, GMLP_GROUPS, GMLP_GROUP_CH)
    causal = jnp.tril(jnp.ones((BLOCK, BLOCK), dtype=w_s.dtype))
    w = w_s * causal[None]
    sv = jnp.einsum('gts,bcsgk->bctgk', w, vb) + jnp.transpose(b_s)[None, None, :, :, None]
    return u * sv.reshape(B, S, GMLP_WIDTH)


def block_causal_attention(q, k, v, map_coef, decay_cum=None):
    B, H, M, S, Dk = q.shape
    Dv = v.shape[-1]
    nb = S // BLOCK
    scale = Dk ** -0.5
    key_pos = jnp.arange(S)

    def one_block(i):
        start = i * BLOCK
        qi = lax.dynamic_slice_in_dim(q, start, BLOCK, axis=3)
        s = jnp.einsum('bhmqd,bhmkd->bhmqk', qi, k,
                       preferred_element_type=jnp.float32) * scale
        if decay_cum is not None:
            cq = lax.dynamic_slice_in_dim(decay_cum, start, BLOCK, axis=2)
            s = s + (cq[:, :, :, None] - decay_cum[:, :, None, :])[:, :, None]
        q_pos = start + jnp.arange(BLOCK)
        mask = key_pos[None, :] <= q_pos[:, None]
        s = jnp.where(mask, s, NEG_INF)
        p = jax.nn.softmax(s, axis=-1)
        p = jnp.einsum('m,bhmqk->bhqk', map_coef, p)
        return jnp.einsum('bhqk,bhkd->bhqd', p.astype(v.dtype), v)

    out = lax.map(one_block, jnp.arange(nb))
    return jnp.moveaxis(out, 0, 2).reshape(B, H, S, Dv)


def setup_inputs(seed: int = 0) -> dict:
    key = jax.random.key(seed)
    ks = jax.random.split(key, 26)
    f32 = jnp.float32

    def nrm(k, shape, scale):
        return jax.random.normal(k, shape, f32) * scale

    L, D = DEPTH, D_MODEL
    return {
        "x": nrm(ks[0], (BATCH, SEQ, D), 1.0),
        "w_in": nrm(ks[1], (L, D, IN_WIDTH), D ** -0.5),
        "b_forget": 3.0 + nrm(ks[2], (L, FOX_HEADS), 0.5),
        "gmlp_ln_g": 1.0 + nrm(ks[3], (L, GMLP_WIDTH), 0.02),
        "gmlp_ln_b": nrm(ks[4], (L, GMLP_WIDTH), 0.02),
        "gmlp_w_s": nrm(ks[5], (L, GMLP_GROUPS, BLOCK, BLOCK), 0.5 * BLOCK ** -0.5),
        "gmlp_b_s": 1.0 + nrm(ks[6], (L, GMLP_GROUPS, BLOCK), 0.02),
        "lam_q1": nrm(ks[7], (L, HEAD_DIM), 0.1),
        "lam_k1": nrm(ks[8], (L, HEAD_DIM), 0.1),
        "lam_q2": nrm(ks[9], (L, HEAD_DIM), 0.1),
        "lam_k2": nrm(ks[10], (L, HEAD_DIM), 0.1),
        "diff_norm_g": 1.0 + nrm(ks[11], (L, DIFF_V_DIM), 0.02),
        "w_branch_a": nrm(ks[12], (L, GMLP_WIDTH, D), GMLP_WIDTH ** -0.5),
        "w_branch_b": nrm(ks[13], (L, FOX_WIDTH, D), FOX_WIDTH ** -0.5),
        "w_branch_c": nrm(ks[14], (L, DIFF_WIDTH, D), DIFF_WIDTH ** -0.5),
        "w_gate": nrm(ks[15], (L, D, N_BRANCH * D), D ** -0.5),
        "b_gate": nrm(ks[16], (L, N_BRANCH * D), 0.02),
        "w_out": nrm(ks[17], (L, D, D), DEEPNORM_BETA * D ** -0.5),
        "ln_mix_g": 1.0 + nrm(ks[18], (L, D), 0.02),
        "ln_mix_b": nrm(ks[19], (L, D), 0.02),
        "w_up": nrm(ks[20], (L, D, D_FF), D ** -0.5),
        "w_down": nrm(ks[21], (L, D_FF, D), DEEPNORM_BETA * D_FF ** -0.5),
        "ln_mlp_g": 1.0 + nrm(ks[22], (L, D), 0.02),
        "ln_mlp_b": nrm(ks[23], (L, D), 0.02),
    }


def reference(x, w_in, b_forget, gmlp_ln_g, gmlp_ln_b, gmlp_w_s, gmlp_b_s,
              lam_q1, lam_k1, lam_q2, lam_k2, diff_norm_g,
              w_branch_a, w_branch_b, w_branch_c, w_gate, b_gate, w_out,
              ln_mix_g, ln_mix_b, w_up, w_down, ln_mlp_g, ln_mlp_b):
    B, S, D = x.shape
    dt = x.dtype

    pos = jnp.arange(S, dtype=jnp.float32)
    inv_freq = ROPE_THETA ** (-jnp.arange(0, HEAD_DIM, 2, dtype=jnp.float32) / HEAD_DIM)
    ang = pos[:, None] * inv_freq[None, :]
    cos = jnp.cos(ang).astype(dt)[None, :, None, None, :]
    sin = jnp.sin(ang).astype(dt)[None, :, None, None, :]
    fox_coef = jnp.ones((1,), jnp.float32)

    for l in range(DEPTH):
        h = x
        proj = h @ w_in[l]

        u = jax.nn.gelu(proj[..., OFF_U:OFF_V], approximate=False)
        v = jax.nn.gelu(proj[..., OFF_V:OFF_FQ], approximate=False)
        out_a = gmlp_spatial_gating(u, v, gmlp_ln_g[l], gmlp_ln_b[l], gmlp_w_s[l], gmlp_b_s[l])

        fq = proj[..., OFF_FQ:OFF_FK].reshape(B, S, FOX_HEADS, HEAD_DIM).transpose(0, 2, 1, 3)[:, :, None]
        fk = proj[..., OFF_FK:OFF_FV].reshape(B, S, FOX_HEADS, HEAD_DIM).transpose(0, 2, 1, 3)[:, :, None]
        fv = proj[..., OFF_FV:OFF_FF].reshape(B, S, FOX_HEADS, HEAD_DIM).transpose(0, 2, 1, 3)
        log_f = jax.nn.log_sigmoid((proj[..., OFF_FF:OFF_DQ] + b_forget[l]).astype(jnp.float32))
        c = jnp.cumsum(log_f, axis=1).transpose(0, 2, 1)
        fo = block_causal_attention(fq, fk, fv, fox_coef, c)
        out_b = fo.transpose(0, 2, 1, 3).reshape(B, S, FOX_WIDTH)

        dq = apply_rope(proj[..., OFF_DQ:OFF_DK].reshape(B, S, DIFF_HEADS, 2, HEAD_DIM), cos, sin)
        dk = apply_rope(proj[..., OFF_DK:OFF_DV].reshape(B, S, DIFF_HEADS, 2, HEAD_DIM), cos, sin)
        dq = dq.transpose(0, 2, 3, 1, 4)
        dk = dk.transpose(0, 2, 3, 1, 4)
        dv = proj[..., OFF_DV:IN_WIDTH].reshape(B, S, DIFF_HEADS, DIFF_V_DIM).transpose(0, 2, 1, 3)
        lam_init = 0.8 - 0.6 * math.exp(-0.3 * l)
        lam = (jnp.exp(jnp.sum(lam_q1[l].astype(jnp.float32) * lam_k1[l].astype(jnp.float32)))
               - jnp.exp(jnp.sum(lam_q2[l].astype(jnp.float32) * lam_k2[l].astype(jnp.float32)))
               + lam_init)
        diff_coef = jnp.stack([jnp.ones((), jnp.float32), -lam])
        do = block_causal_attention(dq, dk, dv, diff_coef)
        do = rms_norm(do, diff_norm_g[l]) * (1.0 - lam_init)
        out_c = do.astype(dt).transpose(0, 2, 1, 3).reshape(B, S, DIFF_WIDTH)

        gates = jax.nn.sigmoid(h @ w_gate[l] + b_gate[l]).reshape(B, S, N_BRANCH, D)
        merged = (gates[:, :, 0] * (out_a @ w_branch_a[l])
                  + gates[:, :, 1] * (out_b @ w_branch_b[l])
                  + gates[:, :, 2] * (out_c @ w_branch_c[l]))
        mix = merged @ w_out[l]
        x = layer_norm(DEEPNORM_ALPHA * x + mix, ln_mix_g[l], ln_mix_b[l])

        ff = jnp.square(jax.nn.relu(x @ w_up[l])) @ w_down[l]
        x = layer_norm(DEEPNORM_ALPHA * x + ff, ln_mlp_g[l], ln_mlp_b[l])

    return x
```

```python
import math
from contextlib import ExitStack

import numpy as np
import concourse.bass as bass
import concourse.mybir as mybir
from concourse.bass_utils import run_bass_kernel_spmd

F32 = mybir.dt.float32
BF16 = mybir.dt.bfloat16
AF = mybir.ActivationFunctionType
ALU = mybir.AluOpType
AX = mybir.AxisListType

ENGS = ("pe", "act", "dve", "pool", "sp")
SEM_CHUNK = 4000
DEPTH = 4
D = 2048
KC = 16
TOK = 1024
OFF_U, OFF_V, OFF_FQ, OFF_FK, OFF_FV, OFF_FF, OFF_DQ, OFF_DK, OFF_DV = 0, 1024, 2048, 3072, 4096, 5120, 5128, 6152, 7176
ALPHA = (2 * DEPTH) ** 0.25
SCALE = 128 ** -0.5
NPV = 112


class Buf:
    __slots__ = ("name", "last_w", "readers")

    def __init__(self, name=""):
        self.name = name
        self.last_w = None
        self.readers = []


class Op:
    __slots__ = ("eng", "fn", "deps", "signal", "idx", "dma_key", "sig_no")

    def __init__(self, eng, fn):
        self.eng = eng
        self.fn = fn
        self.deps = {}
        self.signal = False
        self.idx = None
        self.dma_key = None
        self.sig_no = None


class Sched:
    def __init__(self, nc):
        self.nc = nc
        self.ops = {e: [] for e in ENGS}
        self.dma_counts = {}
        self.dma_inc = {}

    def _add_dep(self, op, prod):
        if prod is None or prod is op:
            return
        if prod.dma_key is not None:
            k = ("d", prod.dma_key)
            v = self.dma_counts[prod.dma_key]
            if op.deps.get(k, 0) < v:
                op.deps[k] = v
        else:
            if prod.eng == "pe" and op.eng == "pe":
                return
            k = ("e", prod.eng)
            cur = op.deps.get(k)
            if cur is None or cur.idx < prod.idx:
                op.deps[k] = prod

    def op(self, eng, fn, reads=(), writes=(), dma_key=None, dma_inc=16):
        o = Op(eng, fn)
        o.idx = len(self.ops[eng])
        for b in reads:
            self._add_dep(o, b.last_w)
        for b in writes:
            self._add_dep(o, b.last_w)
            for r in b.readers:
                self._add_dep(o, r)
        if dma_key is not None:
            o.dma_key = dma_key
            self.dma_inc[dma_key] = dma_inc
            self.dma_counts[dma_key] = self.dma_counts.get(dma_key, 0) + 1
        for b in reads:
            b.readers.append(o)
        for b in writes:
            b.last_w = o
            b.readers = []
        self.ops[eng].append(o)
        return o

    def barrier_all(self):
        last = {}
        for e in ENGS:
            last[e] = None
            for o in reversed(self.ops[e]):
                if o.fn is not None and o.dma_key is None:
                    last[e] = o
                    break
        dk = dict(self.dma_counts)
        for e in ENGS:
            o = Op(e, None)
            o.idx = len(self.ops[e])
            for e2 in ENGS:
                p = last[e2]
                if p is None or (e == "pe" and e2 == "pe"):
                    continue
                o.deps[("e", e2)] = p
            for k, v in dk.items():
                o.deps[("d", k)] = v
            self.ops[e].append(o)

    def emit(self, stack):
        nc = self.nc
        for e in ENGS:
            for o in self.ops[e]:
                for k, v in o.deps.items():
                    if k[0] == "e":
                        v.signal = True
        esems = {}
        for e in ENGS:
            n = 0
            for o in self.ops[e]:
                if o.dma_key is None and o.signal:
                    o.sig_no = n
                    n += 1
            nsem = (n + SEM_CHUNK - 1) // SEM_CHUNK
            esems[e] = [stack.enter_context(nc.semaphore(f"s_{e}{i}")) for i in range(nsem)]
        dsems = {k: stack.enter_context(nc.semaphore(f"d_{k}")) for k in self.dma_counts}
        self.n_sems = sum(len(v) for v in esems.values()) + len(dsems)
        block = stack.enter_context(nc.Block())

        def run(e, engobj):
            waited = {}
            for o in self.ops[e]:
                for k, v in o.deps.items():
                    if k[0] == "e":
                        c, r = divmod(v.sig_no, SEM_CHUNK)
                        sk = ("e", k[1], c)
                        val = r + 1
                        sem = esems[k[1]][c]
                    else:
                        sk = k
                        val = self.dma_inc[k[1]] * v
                        sem = dsems[k[1]]
                    if waited.get(sk, 0) >= val:
                        continue
                    waited[sk] = val
                    engobj.wait_ge(sem, val)
                if o.fn is None:
                    continue
                ins = o.fn(engobj)
                if o.dma_key is not None:
                    ins.then_inc(dsems[o.dma_key], self.dma_inc[o.dma_key])
                elif o.signal:
                    c, r = divmod(o.sig_no, SEM_CHUNK)
                    ins.then_inc(esems[e][c], 1)

        block.tensor(lambda eng: run("pe", eng))
        block.scalar(lambda eng: run("act", eng))
        block.vector(lambda eng: run("dve", eng))
        block.gpsimd(lambda eng: run("pool", eng))
        block.sync(lambda eng: run("sp", eng))


def MM(out, lhsT, rhs, start, stop):
    return lambda e: e.matmul(out, lhsT, rhs, start=start, stop=stop)


def TR(out, in_, ident):
    return lambda e: e.transpose(out, in_, ident)


def ACT(out, in_, func, bias=None, scale=None):
    kw = {}
    if bias is not None:
        kw["bias"] = bias
    if scale is not None:
        kw["scale"] = scale
    return lambda e: e.activation(out=out, in_=in_, func=func, **kw)


def TT(out, a, b, op):
    return lambda e: e.tensor_tensor(out, a, b, op)


def TS(out, a, s1, s2, op0, op1=None):
    if op1 is None:
        return lambda e: e.tensor_scalar(out, a, s1, None, op0)
    return lambda e: e.tensor_scalar(out, a, s1, s2, op0, op1)


def STT(out, a, s, b, op0, op1):
    return lambda e: e.scalar_tensor_tensor(out, a, s, b, op0, op1)


def CP(out, in_):
    return lambda e: e.tensor_copy(out, in_)


def RSUM(out, in_):
    return lambda e: e.reduce_sum(out, in_, AX.X)


def RCP(out, in_):
    return lambda e: e.reciprocal(out, in_)


def MSET(ap, v):
    return lambda e: e.memset(ap, v)


def DMA(out, in_):
    return lambda e: e.dma_start(out=out, in_=in_)


def build(depth=DEPTH, debug=False, stop=None):
    nc = bass.Bass("TRN2", target_bir_lowering=False, num_devices=8)

    def din(name, shape, dt=F32):
        return nc.dram_tensor(name, list(shape), dt, kind="ExternalInput").ap()

    xT = din("xT", [D, TOK])
    w_in = din("w_in", [depth, D, 8200])
    w_gate = din("w_gate", [depth, D, 6144])
    w_br = [din(f"w_br{i}", [depth, 1024, D]) for i in range(3)]
    w_out = din("w_out", [depth, D, D])
    w_up = din("w_up", [depth, D, 8192])
    w_down = din("w_down", [depth, 8192, D])
    pvec = din("pvec", [128, DEPTH * NPV])
    cst = din("cst", [128, 512])
    dmask_d = din("dmask", [128, 512])
    rope_d = din("rope", [128, 2048])
    sel_d = din("sel", [128, 8])
    b_forget = din("b_forget", [DEPTH, 8])
    gln_g = din("gln_g", [DEPTH, 1024])
    gln_b = din("gln_b", [DEPTH, 1024])
    gws = din("gws", [DEPTH, 4, 128, 128])
    gbs = din("gbs", [DEPTH, 4, 128])
    lamv_d = din("lamv", [DEPTH, 512])
    dng = din("dng", [DEPTH, 256])
    outT = nc.dram_tensor("outT", [D, TOK], F32, kind="ExternalOutput").ap()
    dbg = None
    if debug:
        dbg = nc.dram_tensor("dbg", [3072, TOK], F32, kind="ExternalOutput").ap()

    kvloc = nc.dram_tensor("kvloc", [4096, 1024], BF16).ap()
    kvall = nc.dram_tensor("kvall", [4 * 4096, 1024], BF16).ap()
    lfloc = nc.dram_tensor("lfloc", [1024, 8], F32).ap()
    lfall = nc.dram_tensor("lfall", [4096, 8], F32).ap()

    with ExitStack() as st:
        ec = st.enter_context
        xhi = ec(nc.sbuf_tensor("xhi", [128, KC, TOK], BF16))
        xlo = ec(nc.sbuf_tensor("xlo", [128, KC, TOK], BF16))
        ring = [ec(nc.sbuf_tensor(f"ring{i}", [128, 16, 256], BF16)) for i in range(3)]
        SCR = ec(nc.sbuf_tensor("scr", [128, 41024], BF16))
        TMP = ec(nc.sbuf_tensor("tmp", [128, 12288], BF16))
        c16 = ec(nc.sbuf_tensor("c16", [128, 3, 128], BF16))
        c32 = ec(nc.sbuf_tensor("c32", [128, 4, 128], F32))
        dmask = ec(nc.sbuf_tensor("dmsk", [128, 4, 128], BF16))
        pv = ec(nc.sbuf_tensor("pv", [128, DEPTH * NPV], F32))
        bfbc = ec(nc.sbuf_tensor("bfbc", [128, 8], F32))
        lamt = ec(nc.sbuf_tensor("lamt", [128, 512], F32))
        lamr = ec(nc.sbuf_tensor("lamr", [128, 8], F32))
        dngbc = ec(nc.sbuf_tensor("dngbc", [128, 256], F32))
        WmT = ec(nc.sbuf_tensor("WmT", [128, 4, 128], BF16))
        Cc = ec(nc.sbuf_tensor("Cc", [128, 32, 8], F32))
        Ee = ec(nc.sbuf_tensor("Ee", [128, 33, 8], F32))
        small = ec(nc.sbuf_tensor("small", [128, 64], F32))
        selt = ec(nc.sbuf_tensor("selt", [128, 8], F32))
        refE = ec(nc.sbuf_tensor("refE", [128, 8, 8], F32))
        psb = [ec(nc.psum_tensor(f"ps{i}", [128, 512], F32)) for i in range(7)]
        pst = ec(nc.psum_tensor("pst", [128, 1024], BF16))

        S = Sched(nc)
        ident16, maskcp16, ones16 = c16[:, 0, :], c16[:, 1, :], c16[:, 2, :]
        ident32, tri32, ones32, RmT32 = c32[:, 0, :], c32[:, 1, :], c32[:, 2, :], c32[:, 3, :]

        def scr(b0, b1, dt=BF16):
            v = SCR[:, b0 // 2:b1 // 2]
            return v if dt == BF16 else v.bitcast(F32)

        def tmp(b0, b1, dt=BF16):
            v = TMP[:, b0 // 2:b1 // 2]
            return v if dt == BF16 else v.bitcast(F32)

        ps_bufs = [Buf(f"ps{i}") for i in range(7)]
        pst_buf = Buf("pst")
        ps_state = {"i": 0}

        def PS():
            i = ps_state["i"]
            ps_state["i"] = (i + 1) % 7
            return psb[i], ps_bufs[i]

        ring_bufs = [Buf(f"ring{i}") for i in range(3)]
        ring_state = {"i": 0}

        def load_panel(w2d, nkc):
            i = ring_state["i"]
            ring_state["i"] = (i + 1) % 3
            ncols = w2d.shape[1]
            dst = ring[i][:, 0:nkc, 0:ncols]
            S.op("pool", DMA(dst, w2d.rearrange("(c p) n -> p c n", p=128)), writes=[ring_bufs[i]], dma_key=f"w{i}")
            return ring[i], ring_bufs[i]

        xbuf = Buf("x")

        def proj_fm(w2d, nkc, rhs, evac, th_list=(0, 1)):
            ncols = w2d.shape[1]
            for p in range(ncols // 256):
                slot, sb = load_panel(w2d[:, p * 256:(p + 1) * 256], nkc)
                for j in range(2):
                    for th in th_list:
                        ps, pb = PS()
                        for kc in range(nkc):
                            r_ap, r_buf = rhs(kc, th)
                            S.op("pe", MM(ps[:, :], slot[:, kc, j * 128:(j + 1) * 128], r_ap, kc == 0, kc == nkc - 1),
                                 reads=[sb, r_buf], writes=[pb])
                        evac(p * 2 + j, th, ps, pb)

        def x_rhs(kc, th):
            return xhi[:, kc, th * 512:(th + 1) * 512], xbuf

        cbuf = Buf("const")
        S.op("pool", DMA(c16[:, :, :], cst[:, 0:384].rearrange("p (a b) -> p a b", a=3)), writes=[cbuf], dma_key="c")
        S.op("sp", DMA(c32[:, :, :], cst[:, 0:512].rearrange("p (a b) -> p a b", a=4)), writes=[cbuf], dma_key="c")
        S.op("pool", DMA(dmask[:, :, :], dmask_d.rearrange("p (a b) -> p a b", a=4)), writes=[cbuf], dma_key="c")
        S.op("sp", DMA(pv[:, :], pvec), writes=[cbuf], dma_key="c")
        S.op("sp", DMA(selt[:, :], sel_d), writes=[cbuf], dma_key="c")

        xin_t = [tmp(0, 2048, F32), tmp(2048, 4096, F32)]
        xin_b = [Buf("xin0"), Buf("xin1")]
        n = 0
        for c in range(KC):
            for th in range(2):
                t, tb = xin_t[n % 2], xin_b[n % 2]
                S.op("sp", DMA(t, xT[c * 128:(c + 1) * 128, th * 512:(th + 1) * 512]), writes=[tb], dma_key=f"xin{n % 2}")
                S.op("act", ACT(xhi[:, c, th * 512:(th + 1) * 512], t, AF.Copy), reads=[tb], writes=[xbuf])
                S.op("dve", TT(xlo[:, c, th * 512:(th + 1) * 512], t, xhi[:, c, th * 512:(th + 1) * 512], ALU.subtract),
                     reads=[tb, xbuf], writes=[xbuf])
                n += 1

        kvloc_b, kvall_b, lfloc_b, lfall_b = Buf("kvloc"), Buf("kvall"), Buf("lfloc"), Buf("lfall")
        outb = Buf("out")

        class _Stop(Exception):
            pass

        def chk(tag):
            if stop == tag:
                raise _Stop()

        try:
            chk("const")
            for L in range(depth):
                lam_init = 0.8 - 0.6 * math.exp(-0.3 * L)
                pvo = L * NPV
                S.barrier_all()
                lpb = Buf("lparams")
                S.op("sp", DMA(bfbc[:, :].unsqueeze(1), b_forget[L:L + 1, :].partition_broadcast(128)), writes=[lpb], dma_key="lp")
                S.op("sp", DMA(lamt[:, :].unsqueeze(1), lamv_d[L:L + 1, :].partition_broadcast(128)), writes=[lpb], dma_key="lp")
                S.op("sp", DMA(dngbc[:, :].unsqueeze(1), dng[L:L + 1, :].partition_broadcast(128)), writes=[lpb], dma_key="lp")
                S.op("dve", TT(lamt[:, 0:128], lamt[:, 0:128], lamt[:, 128:256], ALU.mult), reads=[lpb], writes=[lpb])
                S.op("dve", TT(lamt[:, 256:384], lamt[:, 256:384], lamt[:, 384:512], ALU.mult), reads=[lpb], writes=[lpb])
                S.op("dve", RSUM(lamr[:, 0:1], lamt[:, 0:128]), reads=[lpb], writes=[lpb])
                S.op("dve", RSUM(lamr[:, 1:2], lamt[:, 256:384]), reads=[lpb], writes=[lpb])
                S.op("act", ACT(lamr[:, 2:4], lamr[:, 0:2], AF.Exp), reads=[lpb], writes=[lpb])
                S.op("dve", TT(lamr[:, 4:5], lamr[:, 3:4], lamr[:, 2:3], ALU.subtract), reads=[lpb], writes=[lpb])
                S.op("dve", TS(lamr[:, 4:5], lamr[:, 4:5], -lam_init, None, ALU.add), reads=[lpb], writes=[lpb])
                S.op("dve", TS(dngbc[:, :], dngbc[:, :], 1.0 - lam_init, None, ALU.mult), reads=[lpb], writes=[lpb])
                neg_lam = lamr[:, 4:5]

                cosT = scr(32768, 36864, F32)
                sinT = scr(36864, 40960, F32)
                ropeb = Buf("rope")
                S.op("sp", DMA(cosT, rope_d[:, 0:1024]), writes=[ropeb], dma_key="lp")
                S.op("sp", DMA(sinT, rope_d[:, 1024:2048]), writes=[ropeb], dma_key="lp")

                rt32 = [tmp(0, 2048, F32), tmp(2048, 4096, F32)]
                rta = [tmp(4096, 6144, F32), tmp(6144, 8192, F32)]
                rtb = [Buf("rt0"), Buf("rt1")]
                rstate = {"i": 0}

                def rope_evac(dst_ap, dst_buf, th, ps, pb):
                    i = rstate["i"]
                    rstate["i"] = (i + 1) % 2
                    t32, ta, tb_ = rt32[i], rta[i], rtb[i]
                    S.op("act", ACT(t32, ps[:, :], AF.Copy), reads=[pb], writes=[tb_])
                    ps2, pb2 = PS()
                    S.op("pe", MM(ps2[:, :], RmT32, t32, True, True), reads=[tb_, cbuf], writes=[pb2])
                    S.op("dve", TT(ta, t32, cosT[:, th * 512:(th + 1) * 512], ALU.mult), reads=[tb_, ropeb], writes=[tb_])
                    S.op("dve", TT(t32, ps2[:, :], sinT[:, th * 512:(th + 1) * 512], ALU.mult), reads=[pb2, ropeb, tb_], writes=[tb_])
                    S.op("dve", TT(dst_ap, ta, t32, ALU.add), reads=[tb_], writes=[dst_buf])

                chk("lp")
                kst = [scr(0, 4096).rearrange("p (a t) -> p a t", a=2), scr(4096, 8192).rearrange("p (a t) -> p a t", a=2)]
                kstb = [Buf("kst0"), Buf("kst1")]
                vst = [scr(8192, 12288).rearrange("p (a t) -> p a t", a=8), scr(12288, 16384).rearrange("p (a t) -> p a t", a=8)]
                vstb = [Buf("vst0"), Buf("vst1")]

                for sec, (col0, roped) in enumerate(((OFF_FK, False), (OFF_DK, True))):
                    def evac_k(oc, th, ps, pb, sec=sec, roped=roped):
                        pn, j = divmod(oc, 2)
                        kb_, ks = kstb[pn % 2], kst[pn % 2]
                        dst = ks[:, j, th * 512:(th + 1) * 512]
                        if roped:
                            rope_evac(dst, kb_, th, ps, pb)
                        else:
                            S.op("act", ACT(dst, ps[:, :], AF.Copy), reads=[pb], writes=[kb_])
                        if j == 1 and th == 1:
                            r0 = sec * 1024 + pn * 256
                            S.op("sp", DMA(kvloc[r0:r0 + 256, :].rearrange("(j p) t -> p j t", p=128), ks[:, :, :]),
                                 reads=[kb_], writes=[kvloc_b], dma_key=f"kst{pn % 2}")
                    proj_fm(w_in[L, :, col0:col0 + 1024], KC, x_rhs, evac_k)

                for sec, col0 in enumerate((OFF_FV, OFF_DV)):
                    for p in range(4):
                        slot, sb = load_panel(w_in[L, :, col0 + p * 256:col0 + (p + 1) * 256], KC)
                        vs, vb_ = vst[p % 2], vstb[p % 2]
                        for tb in range(8):
                            ps, pb = PS()
                            for kc in range(KC):
                                S.op("pe", MM(ps[:, 0:256], xhi[:, kc, tb * 128:(tb + 1) * 128], slot[:, kc, :], kc == 0, kc == KC - 1),
                                     reads=[sb, xbuf], writes=[pb])
                            S.op("act", ACT(vs[:, tb, :], ps[:, 0:256], AF.Copy), reads=[pb], writes=[vb_])
                        r0 = (2 + sec) * 1024
                        S.op("sp", DMA(kvloc[r0:r0 + 1024, p * 256:(p + 1) * 256].rearrange("(tb p) n -> p tb n", p=128), vs[:, :, :]),
                             reads=[vb_], writes=[kvloc_b], dma_key=f"vst{p % 2}")

                slot, sb = load_panel(w_in[L, :, OFF_FF:OFF_FF + 8], KC)
                lfst = small[:, 0:64].rearrange("p (a h) -> p a h", a=8)
                lfb = Buf("lfst")
                for tb in range(8):
                    ps, pb = PS()
                    for kc in range(KC):
                        S.op("pe", MM(ps[:, 0:8], xhi[:, kc, tb * 128:(tb + 1) * 128], slot[:, kc, 0:8], kc == 0, kc == KC - 1),
                             reads=[sb, xbuf], writes=[pb])
                    S.op("dve", TT(lfst[:, tb, :], ps[:, 0:8], bfbc[:, :], ALU.add), reads=[pb, lpb], writes=[lfb])
                S.op("act", ACT(small[:, 0:64], small[:, 0:64], AF.Exp, scale=-1.0), reads=[lfb], writes=[lfb])
                S.op("act", ACT(small[:, 0:64], small[:, 0:64], AF.Ln, bias=1.0), reads=[lfb], writes=[lfb])
                S.op("sp", DMA(lfloc.rearrange("(tb p) h -> p tb h", p=128), lfst), reads=[lfb], writes=[lfloc_b], dma_key="lfst")

                chk("p1")
                for ci in range(8):
                    S.op("pool", (lambda ci: lambda e: e.collective_compute(
                        "AllGather", ALU.bypass, replica_groups=[[0, 1, 2, 3], [4, 5, 6, 7]],
                        ins=[kvloc[ci * 512:(ci + 1) * 512, :]], outs=[kvall[ci * 2048:(ci + 1) * 2048, :]]))(ci),
                        reads=[kvloc_b], writes=[kvall_b], dma_key="cc", dma_inc=1)
                S.op("pool", lambda e: e.collective_compute("AllGather", ALU.bypass, replica_groups=[[0, 1, 2, 3], [4, 5, 6, 7]],
                                                            ins=[lfloc], outs=[lfall]),
                     reads=[lfloc_b], writes=[lfall_b], dma_key="cc", dma_inc=1)

                S.barrier_all()
                chk("cc")
                fq = scr(0, 16384).rearrange("p (h t) -> p h t", h=8)
                dq = scr(16384, 32768).rearrange("p (h t) -> p h t", h=8)
                ua = scr(32768, 49152).rearrange("p (h t) -> p h t", h=8)
                fqb = [[Buf(f"fq{h}_{t}") for t in range(2)] for h in range(8)]
                dqb = [[Buf(f"dq{h}_{t}") for t in range(2)] for h in range(8)]
                uab = [Buf(f"ua{c}") for c in range(8)]

                def evac_fq(oc, th, ps, pb):
                    S.op("act", ACT(fq[:, oc, th * 512:(th + 1) * 512], ps[:, :], AF.Copy), reads=[pb], writes=[fqb[oc][th]])
                proj_fm(w_in[L, :, OFF_FQ:OFF_FQ + 1024], KC, x_rhs, evac_fq)

                def evac_dq(oc, th, ps, pb):
                    rope_evac(dq[:, oc, th * 512:(th + 1) * 512], dqb[oc][th], th, ps, pb)
                proj_fm(w_in[L, :, OFF_DQ:OFF_DQ + 1024], KC, x_rhs, evac_dq)

                S.barrier_all()
                chk("p2a")
                def evac_u(oc, th, ps, pb):
                    S.op("act", ACT(ua[:, oc, th * 512:(th + 1) * 512], ps[:, :], AF.Gelu), reads=[pb], writes=[uab[oc]])
                proj_fm(w_in[L, :, OFF_U:OFF_U + 1024], KC, x_rhs, evac_u)

                chk("g1")
                glnG = tmp(0, 4096, F32)
                glnB = tmp(4096, 8192, F32)
                v32 = tmp(8192, 12288, F32)
                vsq = tmp(12288, 14336, F32)
                vn = [tmp(14336, 16384), tmp(16384, 18432)]
                wraw = tmp(18432, 20480, F32).rearrange("p (g s) -> p g s", g=4)
                bsbc = tmp(20480, 24576, F32).rearrange("p (c t) -> p c t", c=8)
                gpb = Buf("gparams")
                S.op("sp", DMA(glnG.unsqueeze(1), gln_g[L:L + 1, :].partition_broadcast(128)), writes=[gpb], dma_key="lp")
                S.op("sp", DMA(glnB.unsqueeze(1), gln_b[L:L + 1, :].partition_broadcast(128)), writes=[gpb], dma_key="lp")
                for g in range(4):
                    for rep in range(2):
                        S.op("sp", DMA(bsbc[:, 2 * g + rep, :].unsqueeze(1), gbs[L, g:g + 1, :].partition_broadcast(128)), writes=[gpb], dma_key="lp")
                S.op("sp", DMA(wraw, gws[L].rearrange("g t s -> t g s")), writes=[gpb], dma_key="lp")
                wmb = Buf("wm")
                for g in range(4):
                    ps, pb = PS()
                    S.op("pe", MM(ps[:, 0:128], wraw[:, g, :], ident32, True, True), reads=[gpb, cbuf], writes=[pb])
                    S.op("dve", TT(WmT[:, g, :], ps[:, 0:128], tri32, ALU.mult), reads=[pb, cbuf], writes=[wmb])

                chk("g2")
                vpan = scr(49152, 81920).rearrange("p (a c n) -> p a c n", a=4, c=16)
                vpb = Buf("vpan")
                for p in range(4):
                    S.op("pool", DMA(vpan[:, p, :, :], w_in[L, :, OFF_V + p * 256:OFF_V + (p + 1) * 256].rearrange("(c p) n -> p c n", p=128)),
                         writes=[vpb], dma_key="vpan")
                chk("g3")
                v32b, vsqb, stb = Buf("v32"), Buf("vsq"), Buf("gstat")
                vnb = [Buf("vn0"), Buf("vn1")]
                gs = small[:, 0:16]
                for tb in range(8):
                    for half in range(2):
                        ps, pb = PS()
                        for pp in range(2):
                            p = half * 2 + pp
                            for kc in range(KC):
                                S.op("pe", MM(ps[:, pp * 256:(pp + 1) * 256], xhi[:, kc, tb * 128:(tb + 1) * 128], vpan[:, p, kc, :], kc == 0, kc == KC - 1),
                                     reads=[vpb, xbuf], writes=[pb])
                        S.op("act", ACT(v32[:, half * 512:(half + 1) * 512], ps[:, :], AF.Gelu), reads=[pb], writes=[v32b])
                    if tb == 1:
                        chk("g7")
                    if tb == 0:
                        chk("g4")
                    S.op("dve", RSUM(gs[:, 0:1], v32), reads=[v32b], writes=[stb])
                    for half in range(2):
                        S.op("dve", TT(vsq, v32[:, half * 512:(half + 1) * 512], v32[:, half * 512:(half + 1) * 512], ALU.mult), reads=[v32b], writes=[vsqb])
                        S.op("dve", RSUM(gs[:, 1 + half:2 + half], vsq), reads=[vsqb], writes=[stb])
                    S.op("dve", TT(gs[:, 3:4], gs[:, 1:2], gs[:, 2:3], ALU.add), reads=[stb], writes=[stb])
                    S.op("dve", TS(gs[:, 4:5], gs[:, 0:1], 1.0 / 1024, None, ALU.mult), reads=[stb], writes=[stb])
                    S.op("dve", TT(gs[:, 5:6], gs[:, 4:5], gs[:, 4:5], ALU.mult), reads=[stb], writes=[stb])
                    S.op("dve", STT(gs[:, 6:7], gs[:, 3:4], 1.0 / 1024, gs[:, 5:6], ALU.mult, ALU.subtract), reads=[stb], writes=[stb])
                    S.op("dve", TS(gs[:, 7:8], gs[:, 6:7], 1e-5, None, ALU.add), reads=[stb], writes=[stb])
                    S.op("act", ACT(gs[:, 7:8], gs[:, 7:8], AF.Ln), reads=[stb], writes=[stb])
                    S.op("act", ACT(gs[:, 7:8], gs[:, 7:8], AF.Exp, scale=-0.5), reads=[stb], writes=[stb])
                    S.op("dve", TS(v32, v32, gs[:, 4:5], gs[:, 7:8], ALU.subtract, ALU.mult), reads=[stb, v32b], writes=[v32b])
                    S.op("dve", TT(v32, v32, glnG, ALU.mult), reads=[gpb, v32b], writes=[v32b])
                    if tb == 0:
                        chk("g5")
                    vnt, vntb = vn[tb % 2], vnb[tb % 2]
                    S.op("dve", TT(vnt, v32, glnB, ALU.add), reads=[gpb, v32b], writes=[vntb])
                    for bank in range(2):
                        ps, pb = PS()
                        for cc4 in range(4):
                            cc = bank * 4 + cc4
                            S.op("pe", MM(ps[:, cc4 * 128:(cc4 + 1) * 128], vnt[:, cc * 128:(cc + 1) * 128], WmT[:, cc // 2, :], True, True),
                                 reads=[vntb, wmb], writes=[pb])
                        if tb == 0 and bank == 0:
                            chk("g6")
                        for cc4 in range(4):
                            cc = bank * 4 + cc4
                            S.op("dve", TT(vsq[:, 0:128], ps[:, cc4 * 128:(cc4 + 1) * 128], bsbc[:, cc, :], ALU.add), reads=[pb, gpb], writes=[vsqb])
                            S.op("dve", TT(ua[:, cc, tb * 128:(tb + 1) * 128], vsq[:, 0:128], ua[:, cc, tb * 128:(tb + 1) * 128], ALU.mult),
                                 reads=[vsqb, uab[cc]], writes=[uab[cc]])

                S.barrier_all()
                chk("p2b")
                KT = scr(49152, 65536).rearrange("p (m r t) -> p m r t", m=2, r=4)
                Vb = scr(65536, 82048).rearrange("p (k c) -> p k c", c=258)
                ktb = [Buf("kt0"), Buf("kt1")]
                vbb = Buf("vb")
                PT = [tmp(0, 1024), tmp(1024, 2048), tmp(2048, 3072)]
                ptb = [Buf("pt0"), Buf("pt1"), Buf("pt2")]
                otmp = tmp(3072, 7168, F32).rearrange("p (j c) -> p j c", j=4)
                ores = tmp(7168, 11264, F32).rearrange("p (j c) -> p j c", j=4)
                obf = tmp(11264, 13312).rearrange("p (j c) -> p j c", j=4)
                biasT = tmp(13312, 21504, F32).rearrange("p (k m h) -> p k m h", k=32, m=8)
                lftm = tmp(21504, 22528, F32)
                otb, orb, obb, bib, lfb2 = Buf("otmp"), Buf("ores"), Buf("obf"), Buf("bias"), Buf("lftm")
                osm = small[:, 16:48]
                osb = Buf("osm")

                lftm4 = lftm.rearrange("p (g i h) -> p g i h", g=8, i=4)
                for i_ in range(4):
                    S.op("sp", DMA(lftm4[:, :, i_, :], lfall[i_ * 1024:(i_ + 1) * 1024, :].rearrange("(g p) h -> p g h", p=128)),
                         reads=[lfall_b], writes=[lfb2], dma_key="lp")
                ps, pb = PS()
                S.op("pe", MM(ps[:, 0:256], tri32, lftm, True, True), reads=[lfb2, cbuf], writes=[pb])
                ps2, pb2 = PS()
                S.op("pe", MM(ps2[:, 0:256], ones32, lftm, True, True), reads=[lfb2, cbuf], writes=[pb2])
                ceb = Buf("ce")
                S.op("dve", MSET(Ee[:, 0, :], 0.0), writes=[ceb])
                for kb in range(32):
                    S.op("dve", TT(Ee[:, kb + 1, :], Ee[:, kb, :], ps2[:, kb * 8:(kb + 1) * 8], ALU.add), reads=[pb2, ceb], writes=[ceb])
                S.op("dve", TT(Cc[:, :, :], ps[:, 0:256].rearrange("p (k h) -> p k h", k=32), Ee[:, 0:32, :], ALU.add), reads=[pb, ceb], writes=[ceb])
                Ee4 = Ee[:, 0:32, :].rearrange("p (m i) h -> p m i h", i=4)
                S.op("dve", TS(refE[:, :, :], Ee4[:, :, 0, :], selt[:, 0:1], None, ALU.mult), reads=[ceb, cbuf], writes=[bib])
                for i_ in range(1, 4):
                    S.op("dve", STT(refE[:, :, :], Ee4[:, :, i_, :], selt[:, i_:i_ + 1], refE[:, :, :], ALU.mult, ALU.add), reads=[ceb, cbuf, bib], writes=[bib])
                for m in range(8):
                    S.op("dve", TT(biasT[:, :, m, :], Cc[:, :, :], refE[:, m:m + 1, :].broadcast_to([128, 32, 8]), ALU.subtract),
                         reads=[ceb, bib], writes=[bib])
                    S.op("dve", TT(biasT[:, 4 * m:4 * m + 4, m, :], biasT[:, 4 * m:4 * m + 4, m, :],
                                   selt[:, 4:8].unsqueeze(2).broadcast_to([128, 4, 8]), ALU.add), reads=[bib, cbuf], writes=[bib])
                chk("a1")
                pt_state = {"i": 0, "o": 0, "s": 0}

                def attn_unit(kind, u):
                    if kind == "fox":
                        krow0, vrow0, vcol0 = 0, 2048, u * 256
                        qt, qtb = fq, fqb
                    else:
                        krow0, vrow0, vcol0 = 1024, 3072, u * 256
                        qt, qtb = dq, dqb
                    for mp in range(2):
                        hm = 2 * u + mp
                        ci = krow0 // 512 + hm // 4
                        src = kvall[ci * 2048:(ci + 1) * 2048, :].rearrange("(r q) t -> q r t", r=4)[(hm % 4) * 128:(hm % 4 + 1) * 128, :, :]
                        S.op("sp", DMA(KT[:, mp, :, :], src), reads=[kvall_b], writes=[ktb[mp]], dma_key=f"kt{mp}")
                    for r_ in range(4):
                        for half in range(2):
                            ci = vrow0 // 512 + half
                            rows = kvall[ci * 2048 + r_ * 512:ci * 2048 + (r_ + 1) * 512, :]
                            k0 = r_ * 8 + half * 4
                            if kind == "fox":
                                for hh in range(2):
                                    S.op("sp", DMA(Vb[:, k0:k0 + 4, hh * 129:hh * 129 + 128],
                                                   rows[:, vcol0 + hh * 128:vcol0 + (hh + 1) * 128].rearrange("(g p) n -> p g n", p=128)),
                                         reads=[kvall_b], writes=[vbb], dma_key="vb")
                            else:
                                S.op("sp", DMA(Vb[:, k0:k0 + 4, 0:256], rows[:, vcol0:vcol0 + 256].rearrange("(g p) n -> p g n", p=128)),
                                     reads=[kvall_b], writes=[vbb], dma_key="vb")
                    if kind == "fox":
                        S.op("pool", MSET(Vb[:, :, 128:129], 1.0), writes=[vbb])
                        S.op("pool", MSET(Vb[:, :, 257:258], 1.0), writes=[vbb])
                    else:
                        S.op("pool", MSET(Vb[:, :, 256:257], 1.0), writes=[vbb])

                    for m0 in (0, 4):
                        tq = m0 // 4
                        for mp in range(2):
                            hm = 2 * u + mp
                            dvw = 129 if kind == "fox" else 257
                            vc0 = mp * 129 if kind == "fox" else 0
                            if kind == "fox":
                                O = [(psb[b_][:, 0:129], ps_bufs[b_]) for b_ in range(4)]
                            else:
                                O = [(psb[b_][:, 0:257], ps_bufs[b_]) for b_ in range(4)]
                            first = [True] * 4
                            ngroups = m0 + 4
                            for g in range(ngroups):
                                j0 = max(0, g - m0)
                                ncol = (4 - j0) * 128
                                c0 = j0 * 128
                                for i in range(4):
                                    kbi = 4 * g + i
                                    last_kb = (g == ngroups - 1 and i == 3)
                                    sb_ = 4 + pt_state["s"] % 3
                                    pt_state["s"] += 1
                                    ps, pb = psb[sb_], ps_bufs[sb_]
                                    S.op("pe", MM(ps[:, c0:512], KT[:, mp, i, g * 128:(g + 1) * 128], qt[:, hm, m0 * 128 + c0:(m0 + 4) * 128], True, True),
                                         reads=[ktb[mp], qtb[hm][tq]], writes=[pb])
                                    pi = pt_state["i"]
                                    pt_state["i"] = (pi + 1) % 3
                                    pt, ptbuf = PT[pi], ptb[pi]
                                    if kind == "fox":
                                        for j in range(j0, 4):
                                            S.op("act", ACT(pt[:, j * 128:(j + 1) * 128], ps[:, j * 128:(j + 1) * 128], AF.Exp,
                                                            bias=biasT[:, kbi, m0 + j, hm:hm + 1], scale=SCALE),
                                                 reads=[pb, bib], writes=[ptbuf])
                                    else:
                                        S.op("act", ACT(pt[:, c0:512], ps[:, c0:512], AF.Exp, scale=SCALE), reads=[pb], writes=[ptbuf])
                                    if g >= m0:
                                        S.op("pool", TT(pt[:, c0:c0 + 128], pt[:, c0:c0 + 128], dmask[:, i, :], ALU.mult), reads=[ptbuf, cbuf], writes=[ptbuf])
                                    for j in range(j0, 4):
                                        is_last = (4 * (m0 + j) + 3 == kbi)
                                        S.op("pe", MM(O[j][0], pt[:, j * 128:(j + 1) * 128], Vb[:, i * 8 + g, vc0:vc0 + dvw], first[j], is_last),
                                             reads=[ptbuf, vbb], writes=[O[j][1]])
                                        first[j] = False
                            for j in range(4):
                                oj, ojb = O[j]
                                tok0 = (m0 + j) * 128
                                if kind == "fox":
                                    S.op("dve", RCP(osm[:, j:j + 1], oj[:, 128:129]), reads=[ojb], writes=[osb])
                                    S.op("dve", TS(obf[:, j, 0:128], oj[:, 0:128], osm[:, j:j + 1], None, ALU.mult), reads=[ojb, osb], writes=[obb])
                                    S.op("pe", TR(pst[:, j * 128:(j + 1) * 128], obf[:, j, 0:128], ident16), reads=[obb, cbuf], writes=[pst_buf])
                                    S.op("act", ACT(fq[:, hm, tok0:tok0 + 128], pst[:, j * 128:(j + 1) * 128], AF.Copy), reads=[pst_buf], writes=[fqb[hm][tq]])
                                elif mp == 0:
                                    S.op("dve", RCP(osm[:, j:j + 1], oj[:, 256:257]), reads=[ojb], writes=[osb])
                                    S.op("dve", TS(otmp[:, j, :], oj[:, 0:256], osm[:, j:j + 1], None, ALU.mult), reads=[ojb, osb], writes=[otb])
                                else:
                                    S.op("dve", RCP(osm[:, 8 + j:9 + j], oj[:, 256:257]), reads=[ojb], writes=[osb])
                                    S.op("dve", TT(osm[:, 8 + j:9 + j], osm[:, 8 + j:9 + j], neg_lam, ALU.mult), reads=[osb, lpb], writes=[osb])
                                    S.op("dve", STT(ores[:, j, :], oj[:, 0:256], osm[:, 8 + j:9 + j], otmp[:, j, :], ALU.mult, ALU.add),
                                         reads=[ojb, osb, otb], writes=[orb])
                                    S.op("dve", TT(otmp[:, j, :], ores[:, j, :], ores[:, j, :], ALU.mult), reads=[orb], writes=[otb])
                                    S.op("dve", RSUM(osm[:, 16 + j:17 + j], otmp[:, j, :]), reads=[otb], writes=[osb])
                                    S.op("dve", TS(osm[:, 16 + j:17 + j], osm[:, 16 + j:17 + j], 1.0 / 256, 1e-5, ALU.mult, ALU.add), reads=[osb], writes=[osb])
                                    S.op("act", ACT(osm[:, 16 + j:17 + j], osm[:, 16 + j:17 + j], AF.Ln), reads=[osb], writes=[osb])
                                    S.op("act", ACT(osm[:, 16 + j:17 + j], osm[:, 16 + j:17 + j], AF.Exp, scale=-0.5), reads=[osb], writes=[osb])
                                    S.op("dve", STT(obf[:, j, :], ores[:, j, :], osm[:, 16 + j:17 + j], dngbc[:, :], ALU.mult, ALU.mult),
                                         reads=[orb, osb, lpb], writes=[obb])
                                    for c in range(2):
                                        S.op("pe", TR(pst[:, (2 * j + c) * 128:(2 * j + c + 1) * 128], obf[:, j, c * 128:(c + 1) * 128], ident16),
                                             reads=[obb, cbuf], writes=[pst_buf])
                                    for c in range(2):
                                        S.op("act", ACT(dq[:, 2 * u + c, tok0:tok0 + 128], pst[:, (2 * j + c) * 128:(2 * j + c + 1) * 128], AF.Copy),
                                             reads=[pst_buf], writes=[dqb[2 * u + c][tq]])

                for u in range(4):
                    attn_unit("fox", u)
                    if u == 0:
                        chk("a2")
                chk("a3")
                for u in range(4):
                    attn_unit("diff", u)

                S.barrier_all()
                if debug:
                    dt_ = [tmp(0, 2048, F32), tmp(2048, 4096, F32)]
                    db_ = [Buf("dbg0"), Buf("dbg1")]
                    n_ = 0
                    for si_, (src_, bufs_) in enumerate(((ua, None), (fq, fqb), (dq, dqb))):
                        for c_ in range(8):
                            for th_ in range(2):
                                rb_ = uab[c_] if bufs_ is None else bufs_[c_][th_]
                                S.op("act", ACT(dt_[n_ % 2], src_[:, c_, th_ * 512:(th_ + 1) * 512], AF.Copy), reads=[rb_], writes=[db_[n_ % 2]])
                                S.op("sp", DMA(dbg[si_ * 1024 + c_ * 128:si_ * 1024 + (c_ + 1) * 128, th_ * 512:(th_ + 1) * 512], dt_[n_ % 2]),
                                     reads=[db_[n_ % 2]], writes=[outb], dma_key=f"dbg{n_ % 2}")
                                n_ += 1
                chk("p3")
                merged = scr(49152, 81920).rearrange("p (c t) -> p c t", c=16)
                mgb = [Buf(f"mg{c}") for c in range(16)]
                gsig = [tmp(0, 4096).rearrange("p (a t) -> p a t", a=2), tmp(4096, 8192).rearrange("p (a t) -> p a t", a=2)]
                gsb = [Buf("gs0"), Buf("gs1")]
                macc = tmp(8192, 16384, F32).rearrange("p (a t) -> p a t", a=2)
                mab = [[Buf(f"ma{a}{t}") for t in range(2)] for a in range(2)]
                tmpm = [tmp(16384, 18432, F32), tmp(18432, 20480, F32)]
                tmb = [Buf("tm0"), Buf("tm1")]
                tstate = {"i": 0}
                for ocp in range(8):
                    for bi in range(3):
                        gs_t, gs_b = gsig[bi % 2], gsb[bi % 2]

                        def evac_gate(oc, th, ps, pb, bi=bi, gs_t=gs_t, gs_b=gs_b, ocp=ocp):
                            col = pvo + bi * 16 + ocp * 2 + oc
                            S.op("act", ACT(gs_t[:, oc, th * 512:(th + 1) * 512], ps[:, :], AF.Sigmoid, bias=pv[:, col:col + 1]),
                                 reads=[pb, cbuf], writes=[gs_b])
                        proj_fm(w_gate[L, :, bi * 2048 + ocp * 256:bi * 2048 + (ocp + 1) * 256], KC, x_rhs, evac_gate)

                        src_t = (ua, fq, dq)[bi]

                        def br_rhs(kc, th, bi=bi, src_t=src_t):
                            if bi == 0:
                                b = uab[kc]
                            elif bi == 1:
                                b = fqb[kc][th]
                            else:
                                b = dqb[kc][th]
                            return src_t[:, kc, th * 512:(th + 1) * 512], b

                        def evac_br(oc, th, ps, pb, bi=bi, gs_t=gs_t, gs_b=gs_b, ocp=ocp):
                            g_ap = gs_t[:, oc, th * 512:(th + 1) * 512]
                            acc = macc[:, oc, th * 512:(th + 1) * 512]
                            ab = mab[oc][th]
                            if bi == 0:
                                S.op("dve", TT(acc, ps[:, :], g_ap, ALU.mult), reads=[pb, gs_b], writes=[ab])
                            else:
                                ti = tstate["i"]
                                tstate["i"] = (ti + 1) % 2
                                S.op("dve", TT(tmpm[ti], ps[:, :], g_ap, ALU.mult), reads=[pb, gs_b], writes=[tmb[ti]])
                                if bi == 1:
                                    S.op("pool", TT(acc, acc, tmpm[ti], ALU.add), reads=[tmb[ti], ab], writes=[ab])
                                else:
                                    S.op("pool", TT(merged[:, ocp * 2 + oc, th * 512:(th + 1) * 512], acc, tmpm[ti], ALU.add),
                                         reads=[tmb[ti], ab], writes=[mgb[ocp * 2 + oc]])
                        proj_fm(w_br[bi][L, :, ocp * 256:(ocp + 1) * 256], 8, br_rhs, evac_br)

                S.barrier_all()

                chk("p4")
                mean_t = tmp(0, 2048, F32)
                rstd_t = tmp(2048, 4096, F32)
                sqt = [tmp(4096, 6144, F32), tmp(6144, 8192, F32)]
                y32 = [tmp(8192, 10240, F32), tmp(10240, 12288, F32)]
                sqb = [Buf("sq0"), Buf("sq1")]
                y32b = [Buf("y0"), Buf("y1")]
                stat_b = Buf("lnstat")

                def layer_norm(pre, preb, th, gcol, bcol, final):
                    ps_s, pb_s = PS()
                    ps_q, pb_q = PS()
                    for c in range(KC):
                        S.op("pe", MM(ps_s[:, :], ones32, pre[:, c, :], c == 0, c == KC - 1), reads=[preb, cbuf], writes=[pb_s])
                    for c in range(KC):
                        S.op("act", ACT(sqt[c % 2], pre[:, c, :], AF.Square), reads=[preb], writes=[sqb[c % 2]])
                        S.op("pe", MM(ps_q[:, :], ones32, sqt[c % 2], c == 0, c == KC - 1), reads=[sqb[c % 2], cbuf], writes=[pb_q])
                    S.op("dve", TS(mean_t, ps_s[:, :], 1.0 / D, None, ALU.mult), reads=[pb_s], writes=[stat_b])
                    S.op("dve", TT(rstd_t, mean_t, mean_t, ALU.mult), reads=[stat_b], writes=[stat_b])
                    S.op("dve", STT(rstd_t, ps_q[:, :], 1.0 / D, rstd_t, ALU.mult, ALU.subtract), reads=[pb_q, stat_b], writes=[stat_b])
                    S.op("dve", TS(rstd_t, rstd_t, 1e-5, None, ALU.add), reads=[stat_b], writes=[stat_b])
                    S.op("act", ACT(rstd_t, rstd_t, AF.Ln), reads=[stat_b], writes=[stat_b])
                    S.op("act", ACT(rstd_t, rstd_t, AF.Exp, scale=-0.5), reads=[stat_b], writes=[stat_b])
                    for c in range(KC):
                        y, yb = y32[c % 2], y32b[c % 2]
                        S.op("dve", TT(y, pre[:, c, :], mean_t, ALU.subtract), reads=[preb, stat_b], writes=[yb])
                        S.op("dve", TT(y, y, rstd_t, ALU.mult), reads=[stat_b, yb], writes=[yb])
                        S.op("dve", TS(y, y, pv[:, gcol + c:gcol + c + 1], pv[:, bcol + c:bcol + c + 1], ALU.mult, ALU.add), reads=[cbuf, yb], writes=[yb])
                        if final:
                            S.op("sp", DMA(outT[c * 128:(c + 1) * 128, th * 512:(th + 1) * 512], y), reads=[yb], writes=[outb], dma_key=f"out{c % 2}")
                        else:
                            hi = xhi[:, c, th * 512:(th + 1) * 512]
                            S.op("act", ACT(hi, y, AF.Copy), reads=[yb], writes=[xbuf])
                            S.op("pool", TT(xlo[:, c, th * 512:(th + 1) * 512], y, hi, ALU.subtract), reads=[yb, xbuf], writes=[xbuf])

                preh = scr(0, 32768, F32).rearrange("p (c t) -> p c t", c=16)
                for th in range(2):
                    prehb = Buf(f"preh{th}")

                    def mg_rhs(kc, th_, th=th):
                        return merged[:, kc, th * 512:(th + 1) * 512], mgb[kc]

                    def evac_mix(oc, th_, ps, pb, th=th, prehb=prehb):
                        ti = tstate["i"]
                        tstate["i"] = (ti + 1) % 2
                        t_, tb_ = tmpm[ti], tmb[ti]
                        S.op("pool", TT(t_, xhi[:, oc, th * 512:(th + 1) * 512], xlo[:, oc, th * 512:(th + 1) * 512], ALU.add), reads=[xbuf], writes=[tb_])
                        S.op("dve", STT(preh[:, oc, :], t_, ALPHA, ps[:, :], ALU.mult, ALU.add), reads=[tb_, pb], writes=[prehb])
                    proj_fm(w_out[L, :, :], KC, mg_rhs, evac_mix, th_list=(th,))
                    layer_norm(preh, prehb, th, pvo + 48, pvo + 64, False)
                    S.barrier_all()

                chk("p5")
                pre = scr(0, 65536, F32).rearrange("p (c t) -> p c t", c=16)
                hT = scr(65536, 81920).rearrange("p (c t) -> p c t", c=8)
                preb = [Buf("pre0"), Buf("pre1")]
                hb = [Buf(f"h{c}") for c in range(8)]
                for q in range(8):
                    def evac_up(oc, th, ps, pb, q=q):
                        ti = tstate["i"]
                        tstate["i"] = (ti + 1) % 2
                        S.op("act", ACT(tmpm[ti], ps[:, :], AF.Relu), reads=[pb], writes=[tmb[ti]])
                        S.op("dve", TT(hT[:, oc, th * 512:(th + 1) * 512], tmpm[ti], tmpm[ti], ALU.mult), reads=[tmb[ti]], writes=[hb[oc]])
                    proj_fm(w_up[L, :, q * 1024:(q + 1) * 1024], KC, x_rhs, evac_up)

                    def h_rhs(kc, th):
                        return hT[:, kc, th * 512:(th + 1) * 512], hb[kc]

                    def evac_dn(oc, th, ps, pb, q=q):
                        dst = pre[:, oc, th * 512:(th + 1) * 512]
                        if q == 0:
                            ti = tstate["i"]
                            tstate["i"] = (ti + 1) % 2
                            t_, tb_ = tmpm[ti], tmb[ti]
                            S.op("pool", TT(t_, xhi[:, oc, th * 512:(th + 1) * 512], xlo[:, oc, th * 512:(th + 1) * 512], ALU.add), reads=[xbuf], writes=[tb_])
                            S.op("dve", STT(dst, t_, ALPHA, ps[:, :], ALU.mult, ALU.add), reads=[tb_, pb], writes=[preb[th]])
                        else:
                            S.op("dve", TT(dst, dst, ps[:, :], ALU.add), reads=[pb, preb[th]], writes=[preb[th]])
                    proj_fm(w_down[L, q * 1024:(q + 1) * 1024, :], 8, h_rhs, evac_dn)
                S.barrier_all()
                for th in range(2):
                    layer_norm(pre[:, :, th * 512:(th + 1) * 512], preb[th], th, pvo + 80, pvo + 96, L == depth - 1)

        except _Stop:
            pass
        S.barrier_all()
        S.emit(st)
        nops = sum(len(v) for v in S.ops.values())
        print(f"[kernel] ops={nops} sems={S.n_sems}", flush=True)
    return nc


def _consts():
    ident = np.eye(128, dtype=np.float32)
    p = np.arange(128)[:, None]
    c = np.arange(128)[None, :]
    tri = (c >= p).astype(np.float32)
    ones = np.ones((128, 128), np.float32)
    rmt = np.zeros((128, 128), np.float32)
    for d in range(64):
        rmt[d + 64, d] = -1.0
        rmt[d, d + 64] = 1.0
    return np.concatenate([ident, tri, ones, rmt], axis=1)


def _dmask(r):
    p = np.arange(128)[:, None]
    c = np.arange(128)[None, :]
    tri = (c >= p).astype(np.float32)
    out = np.zeros((128, 4, 128), np.float32)
    for i in range(4):
        if i < r:
            out[:, i, :] = 1.0
        elif i == r:
            out[:, i, :] = tri
    return out.reshape(128, 512)


def _rope(r):
    pos = np.concatenate([(r + 4 * m) * 128 + np.arange(128) for m in range(8)]).astype(np.float32)
    inv_freq = (10000.0 ** (-np.arange(0, 128, 2, dtype=np.float32) / 128)).astype(np.float32)
    ang = pos[None, :] * inv_freq[:, None]
    cos = np.cos(ang).astype(np.float32)
    sin = np.sin(ang).astype(np.float32)
    return np.concatenate([np.concatenate([cos, cos], 0), np.concatenate([sin, sin], 0)], axis=1)


_NC_CACHE = {}


def kernel(x, w_in, b_forget, gmlp_ln_g, gmlp_ln_b, gmlp_w_s, gmlp_b_s, lam_q1, lam_k1, lam_q2, lam_k2, diff_norm_g,
           w_branch_a, w_branch_b, w_branch_c, w_gate, b_gate, w_out, ln_mix_g, ln_mix_b, w_up, w_down, ln_mlp_g, ln_mlp_b,
           _depth=DEPTH, _debug=False, _stop=None):
    f = lambda a: np.ascontiguousarray(np.asarray(a, dtype=np.float32))
    x = f(x)
    key = (_depth, _debug, _stop)
    if key not in _NC_CACHE:
        _NC_CACHE[key] = build(_depth, _debug, _stop)
    nc = _NC_CACHE[key]

    def fm(v):
        v = f(v)
        return v.reshape(DEPTH, -1, 128).transpose(2, 0, 1)
    pvec = np.concatenate([fm(b_gate), fm(ln_mix_g), fm(ln_mix_b), fm(ln_mlp_g), fm(ln_mlp_b)], axis=2)
    pvec = np.ascontiguousarray(pvec.reshape(128, DEPTH * NPV))
    lamv = np.ascontiguousarray(np.concatenate([f(lam_q1), f(lam_k1), f(lam_q2), f(lam_k2)], axis=1))
    shared = {
        "w_in": f(w_in[:_depth]), "w_gate": f(w_gate[:_depth]), "w_br0": f(w_branch_a[:_depth]), "w_br1": f(w_branch_b[:_depth]),
        "w_br2": f(w_branch_c[:_depth]), "w_out": f(w_out[:_depth]), "w_up": f(w_up[:_depth]), "w_down": f(w_down[:_depth]), "pvec": pvec, "cst": _consts(),
        "b_forget": f(b_forget), "gln_g": f(gmlp_ln_g), "gln_b": f(gmlp_ln_b), "gws": f(gmlp_w_s), "gbs": f(gmlp_b_s),
        "lamv": lamv, "dng": f(diff_norm_g),
    }
    in_maps = []
    for c in range(8):
        b, r = divmod(c, 4)
        blocks = [x[b, (r + 4 * m) * 128:(r + 4 * m + 1) * 128, :] for m in range(8)]
        xs = np.concatenate(blocks, axis=0)
        m = dict(shared)
        m["xT"] = np.ascontiguousarray(xs.T)
        m["dmask"] = _dmask(r)
        m["rope"] = _rope(r)
        sel = np.zeros((128, 8), np.float32)
        sel[:, r] = 1.0
        sel[:, 4 + r + 1:8] = -30000.0
        m["sel"] = sel
        in_maps.append(m)
    res = run_bass_kernel_spmd(nc, in_maps, core_ids=list(range(8)))
    out = np.empty((2, 4096, 2048), np.float32)
    for c in range(8):
        b, r = divmod(c, 4)
        o = res.results[c]["outT"].T
        for m in range(8):
            out[b, (r + 4 * m) * 128:(r + 4 * m + 1) * 128, :] = o[m * 128:(m + 1) * 128, :]
    if _debug:
        return out, res
    return out
```

```python
import math
from contextlib import ExitStack

import numpy as np
import concourse.bass as bass
import concourse.mybir as mybir
from concourse.bass_utils import run_bass_kernel_spmd

F32 = mybir.dt.float32
BF16 = mybir.dt.bfloat16
AF = mybir.ActivationFunctionType
ALU = mybir.AluOpType
AX = mybir.AxisListType

ENGS = ("pe", "act", "dve", "pool", "sp")
SEM_CHUNK = 4000
DEPTH = 4
D = 2048
KC = 16
TOK = 1024
OFF_U, OFF_V, OFF_FQ, OFF_FK, OFF_FV, OFF_FF, OFF_DQ, OFF_DK, OFF_DV = 0, 1024, 2048, 3072, 4096, 5120, 5128, 6152, 7176
ALPHA = (2 * DEPTH) ** 0.25
SCALE = 128 ** -0.5
NPV = 112


class Buf:
    __slots__ = ("name", "writers", "readers")

    def __init__(self, name=""):
        self.name = name
        self.writers = []
        self.readers = []


class Op:
    __slots__ = ("eng", "fn", "deps", "signal", "idx", "dma_key", "sig_no")

    def __init__(self, eng, fn):
        self.eng = eng
        self.fn = fn
        self.deps = {}
        self.signal = False
        self.idx = None
        self.dma_key = None
        self.sig_no = None


class Sched:
    def __init__(self, nc):
        self.nc = nc
        self.ops = {e: [] for e in ENGS}
        self.dma_counts = {}
        self.dma_inc = {}

    def _add_dep(self, op, prod):
        if prod is None or prod is op:
            return
        if prod.dma_key is not None:
            k = ("d", prod.dma_key)
            v = self.dma_counts[prod.dma_key]
            if op.deps.get(k, 0) < v:
                op.deps[k] = v
        else:
            if prod.eng == "pe" and op.eng == "pe":
                return
            k = ("e", prod.eng)
            cur = op.deps.get(k)
            if cur is None or cur.idx < prod.idx:
                op.deps[k] = prod

    def op(self, eng, fn, reads=(), writes=(), dma_key=None, dma_inc=16):
        o = Op(eng, fn)
        o.idx = len(self.ops[eng])
        for b in reads:
            for w in b.writers:
                self._add_dep(o, w)
        for b in writes:
            for w in b.writers:
                self._add_dep(o, w)
            for r in b.readers:
                self._add_dep(o, r)
        if dma_key is not None:
            o.dma_key = dma_key
            self.dma_inc[dma_key] = dma_inc
            self.dma_counts[dma_key] = self.dma_counts.get(dma_key, 0) + 1
        for b in reads:
            b.readers.append(o)
        for b in writes:
            if b.readers:
                b.writers = [o]
                b.readers = []
            elif not b.writers or b.writers[-1] is not o:
                kk = o.dma_key if o.dma_key is not None else o.eng
                b.writers = [w for w in b.writers if (w.dma_key if w.dma_key is not None else w.eng) != kk]
                b.writers.append(o)
        self.ops[eng].append(o)
        return o

    def barrier_all(self, skip=()):
        last = {}
        for e in ENGS:
            last[e] = None
            for o in reversed(self.ops[e]):
                if o.fn is not None and o.dma_key is None:
                    last[e] = o
                    break
        dk = dict(self.dma_counts)
        for e in ENGS:
            o = Op(e, None)
            o.idx = len(self.ops[e])
            for e2 in ENGS:
                p = last[e2]
                if p is None or (e == "pe" and e2 == "pe"):
                    continue
                o.deps[("e", e2)] = p
            for k, v in dk.items():
                if k not in skip:
                    o.deps[("d", k)] = v
            self.ops[e].append(o)

    def emit(self, stack):
        nc = self.nc
        for e in ENGS:
            for o in self.ops[e]:
                for k, v in o.deps.items():
                    if k[0] == "e":
                        v.signal = True
        esems = {}
        for e in ENGS:
            n = 0
            for o in self.ops[e]:
                if o.dma_key is None and o.signal:
                    o.sig_no = n
                    n += 1
            nsem = (n + SEM_CHUNK - 1) // SEM_CHUNK
            esems[e] = [stack.enter_context(nc.semaphore(f"s_{e}{i}")) for i in range(nsem)]
        dsems = {k: stack.enter_context(nc.semaphore(f"d_{k}")) for k in self.dma_counts}
        self.n_sems = sum(len(v) for v in esems.values()) + len(dsems)
        block = stack.enter_context(nc.Block())

        def run(e, engobj):
            waited = {}
            for o in self.ops[e]:
                for k, v in o.deps.items():
                    if k[0] == "e":
                        c, r = divmod(v.sig_no, SEM_CHUNK)
                        sk = ("e", k[1], c)
                        val = r + 1
                        sem = esems[k[1]][c]
                    else:
                        sk = k
                        val = self.dma_inc[k[1]] * v
                        sem = dsems[k[1]]
                    if waited.get(sk, 0) >= val:
                        continue
                    waited[sk] = val
                    engobj.wait_ge(sem, val)
                if o.fn is None:
                    continue
                ins = o.fn(engobj)
                if o.dma_key is not None:
                    ins.then_inc(dsems[o.dma_key], self.dma_inc[o.dma_key])
                elif o.signal:
                    c, r = divmod(o.sig_no, SEM_CHUNK)
                    ins.then_inc(esems[e][c], 1)

        block.tensor(lambda eng: run("pe", eng))
        block.scalar(lambda eng: run("act", eng))
        block.vector(lambda eng: run("dve", eng))
        block.gpsimd(lambda eng: run("pool", eng))
        block.sync(lambda eng: run("sp", eng))


def MM(out, lhsT, rhs, start, stop):
    return lambda e: e.matmul(out, lhsT, rhs, start=start, stop=stop)


def TR(out, in_, ident):
    return lambda e: e.transpose(out, in_, ident)


def ACT(out, in_, func, bias=None, scale=None):
    kw = {}
    if bias is not None:
        kw["bias"] = bias
    if scale is not None:
        kw["scale"] = scale
    return lambda e: e.activation(out=out, in_=in_, func=func, **kw)


def TT(out, a, b, op):
    return lambda e: e.tensor_tensor(out, a, b, op)


def TS(out, a, s1, s2, op0, op1=None):
    if op1 is None:
        return lambda e: e.tensor_scalar(out, a, s1, None, op0)
    return lambda e: e.tensor_scalar(out, a, s1, s2, op0, op1)


def STT(out, a, s, b, op0, op1):
    return lambda e: e.scalar_tensor_tensor(out, a, s, b, op0, op1)


def CP(out, in_):
    return lambda e: e.tensor_copy(out, in_)


def RSUM(out, in_):
    return lambda e: e.reduce_sum(out, in_, AX.X)


def RCP(out, in_):
    return lambda e: e.reciprocal(out, in_)


def MSET(ap, v):
    return lambda e: e.memset(ap, v)


def DMA(out, in_):
    return lambda e: e.dma_start(out=out, in_=in_)


def build(depth=DEPTH, debug=False, stop=None):
    nc = bass.Bass("TRN2", target_bir_lowering=False, num_devices=8)

    def din(name, shape, dt=F32):
        return nc.dram_tensor(name, list(shape), dt, kind="ExternalInput").ap()

    xT = din("xT", [D, TOK])
    w_in = din("w_in", [depth, D, 8200])
    w_gate = din("w_gate", [depth, D, 6144])
    w_br = [din(f"w_br{i}", [depth, 1024, D]) for i in range(3)]
    w_out = din("w_out", [depth, D, D])
    w_up = din("w_up", [depth, D, 8192])
    w_down = din("w_down", [depth, 8192, D])
    pvec = din("pvec", [128, DEPTH * NPV])
    cst = din("cst", [128, 512])
    dmask_d = din("dmask", [128, 512])
    rope_d = din("rope", [128, 2048])
    sel_d = din("sel", [128, 8])
    b_forget = din("b_forget", [DEPTH, 8])
    gln_g = din("gln_g", [DEPTH, 1024])
    gln_b = din("gln_b", [DEPTH, 1024])
    gws = din("gws", [DEPTH, 4, 128, 128])
    gbs = din("gbs", [DEPTH, 4, 128])
    lamv_d = din("lamv", [DEPTH, 512])
    dng = din("dng", [DEPTH, 256])
    outT = nc.dram_tensor("outT", [D, TOK], F32, kind="ExternalOutput").ap()
    dbg = None
    if debug:
        dbg = nc.dram_tensor("dbg", [3072, TOK], F32, kind="ExternalOutput").ap()

    kvloc = nc.dram_tensor("kvloc", [4096, 1024], BF16).ap()
    kvall = nc.dram_tensor("kvall", [4 * 4096, 1024], BF16).ap()
    lfloc = nc.dram_tensor("lfloc", [1024, 8], F32).ap()
    lfall = nc.dram_tensor("lfall", [4096, 8], F32).ap()

    with ExitStack() as st:
        ec = st.enter_context
        xhi = ec(nc.sbuf_tensor("xhi", [128, KC, TOK], BF16))
        xlo = ec(nc.sbuf_tensor("xlo", [128, KC, TOK], BF16))
        ring = [ec(nc.sbuf_tensor(f"ring{i}", [128, 16, 256], BF16)) for i in range(3)]
        SCR = ec(nc.sbuf_tensor("scr", [128, 41024], BF16))
        TMP = ec(nc.sbuf_tensor("tmp", [128, 12288], BF16))
        c16 = ec(nc.sbuf_tensor("c16", [128, 3, 128], BF16))
        c32 = ec(nc.sbuf_tensor("c32", [128, 4, 128], F32))
        dmask = ec(nc.sbuf_tensor("dmsk", [128, 4, 128], BF16))
        pv = ec(nc.sbuf_tensor("pv", [128, DEPTH * NPV], F32))
        bfbc = ec(nc.sbuf_tensor("bfbc", [128, 8], F32))
        lamt = ec(nc.sbuf_tensor("lamt", [128, 512], F32))
        lamr = ec(nc.sbuf_tensor("lamr", [128, 8], F32))
        dngbc = ec(nc.sbuf_tensor("dngbc", [128, 256], F32))
        WmT = ec(nc.sbuf_tensor("WmT", [128, 4, 128], BF16))
        Cc = ec(nc.sbuf_tensor("Cc", [128, 32, 8], F32))
        Ee = ec(nc.sbuf_tensor("Ee", [128, 33, 8], F32))
        small = ec(nc.sbuf_tensor("small", [128, 64], F32))
        selt = ec(nc.sbuf_tensor("selt", [128, 8], F32))
        refE = ec(nc.sbuf_tensor("refE", [128, 8, 8], F32))
        psb = [ec(nc.psum_tensor(f"ps{i}", [128, 512], F32)) for i in range(7)]
        pst = ec(nc.psum_tensor("pst", [128, 1024], BF16))

        S = Sched(nc)
        ident16, maskcp16, ones16 = c16[:, 0, :], c16[:, 1, :], c16[:, 2, :]
        ident32, tri32, ones32, RmT32 = c32[:, 0, :], c32[:, 1, :], c32[:, 2, :], c32[:, 3, :]

        def scr(b0, b1, dt=BF16):
            v = SCR[:, b0 // 2:b1 // 2]
            return v if dt == BF16 else v.bitcast(F32)

        def tmp(b0, b1, dt=BF16):
            v = TMP[:, b0 // 2:b1 // 2]
            return v if dt == BF16 else v.bitcast(F32)

        ps_bufs = [Buf(f"ps{i}") for i in range(7)]
        pst_buf = Buf("pst")
        ps_state = {"i": 0}

        def PS():
            i = ps_state["i"]
            ps_state["i"] = (i + 1) % 7
            return psb[i], ps_bufs[i]

        ring_bufs = [Buf(f"ring{i}") for i in range(3)]
        ring_state = {"i": 0}

        def load_panel(w2d, nkc):
            i = ring_state["i"]
            ring_state["i"] = (i + 1) % 3
            ncols = w2d.shape[1]
            dst = ring[i][:, 0:nkc, 0:ncols]
            S.op("pool", DMA(dst, w2d.rearrange("(c p) n -> p c n", p=128)), writes=[ring_bufs[i]], dma_key=f"w{i}")
            return ring[i], ring_bufs[i]

        xbuf = Buf("x")

        def proj_fm(w2d, nkc, rhs, evac, th_list=(0, 1)):
            ncols = w2d.shape[1]
            for p in range(ncols // 256):
                slot, sb = load_panel(w2d[:, p * 256:(p + 1) * 256], nkc)
                for j in range(2):
                    for th in th_list:
                        ps, pb = PS()
                        for kc in range(nkc):
                            r_ap, r_buf = rhs(kc, th)
                            S.op("pe", MM(ps[:, :], slot[:, kc, j * 128:(j + 1) * 128], r_ap, kc == 0, kc == nkc - 1),
                                 reads=[sb, r_buf], writes=[pb])
                        evac(p * 2 + j, th, ps, pb)

        def x_rhs(kc, th):
            return xhi[:, kc, th * 512:(th + 1) * 512], xbuf

        cbuf = Buf("const")
        S.op("pool", DMA(c16[:, :, :], cst[:, 0:384].rearrange("p (a b) -> p a b", a=3)), writes=[cbuf], dma_key="c")
        S.op("sp", DMA(c32[:, :, :], cst[:, 0:512].rearrange("p (a b) -> p a b", a=4)), writes=[cbuf], dma_key="c")
        S.op("pool", DMA(dmask[:, :, :], dmask_d.rearrange("p (a b) -> p a b", a=4)), writes=[cbuf], dma_key="c")
        S.op("sp", DMA(pv[:, :], pvec), writes=[cbuf], dma_key="c")
        S.op("sp", DMA(selt[:, :], sel_d), writes=[cbuf], dma_key="c")

        xin_t = [tmp(0, 2048, F32), tmp(2048, 4096, F32)]
        xin_b = [Buf("xin0"), Buf("xin1")]
        n = 0
        for c in range(KC):
            for th in range(2):
                t, tb = xin_t[n % 2], xin_b[n % 2]
                S.op("sp", DMA(t, xT[c * 128:(c + 1) * 128, th * 512:(th + 1) * 512]), writes=[tb], dma_key=f"xin{n % 2}")
                S.op("act", ACT(xhi[:, c, th * 512:(th + 1) * 512], t, AF.Copy), reads=[tb], writes=[xbuf])
                S.op("dve", TT(xlo[:, c, th * 512:(th + 1) * 512], t, xhi[:, c, th * 512:(th + 1) * 512], ALU.subtract),
                     reads=[tb, xbuf], writes=[xbuf])
                n += 1

        kvloc_b, kvall_b, lfloc_b, lfall_b = [Buf(f"kvloc{i}") for i in range(8)], Buf("kvall"), Buf("lfloc"), Buf("lfall")
        outb = Buf("out")

        class _Stop(Exception):
            pass

        def chk(tag):
            if stop == tag:
                raise _Stop()

        try:
            chk("const")
            for L in range(depth):
                lam_init = 0.8 - 0.6 * math.exp(-0.3 * L)
                pvo = L * NPV
                S.barrier_all()
                lpb = Buf("lparams")
                S.op("sp", DMA(bfbc[:, :].unsqueeze(1), b_forget[L:L + 1, :].partition_broadcast(128)), writes=[lpb], dma_key="lp")
                S.op("sp", DMA(lamt[:, :].unsqueeze(1), lamv_d[L:L + 1, :].partition_broadcast(128)), writes=[lpb], dma_key="lp")
                S.op("sp", DMA(dngbc[:, :].unsqueeze(1), dng[L:L + 1, :].partition_broadcast(128)), writes=[lpb], dma_key="lp")
                S.op("dve", TT(lamt[:, 0:128], lamt[:, 0:128], lamt[:, 128:256], ALU.mult), reads=[lpb], writes=[lpb])
                S.op("dve", TT(lamt[:, 256:384], lamt[:, 256:384], lamt[:, 384:512], ALU.mult), reads=[lpb], writes=[lpb])
                S.op("dve", RSUM(lamr[:, 0:1], lamt[:, 0:128]), reads=[lpb], writes=[lpb])
                S.op("dve", RSUM(lamr[:, 1:2], lamt[:, 256:384]), reads=[lpb], writes=[lpb])
                S.op("act", ACT(lamr[:, 2:4], lamr[:, 0:2], AF.Exp), reads=[lpb], writes=[lpb])
                S.op("dve", TT(lamr[:, 4:5], lamr[:, 3:4], lamr[:, 2:3], ALU.subtract), reads=[lpb], writes=[lpb])
                S.op("dve", TS(lamr[:, 4:5], lamr[:, 4:5], -lam_init, None, ALU.add), reads=[lpb], writes=[lpb])
                S.op("dve", TS(dngbc[:, :], dngbc[:, :], 1.0 - lam_init, None, ALU.mult), reads=[lpb], writes=[lpb])
                neg_lam = lamr[:, 4:5]

                cosT = scr(32768, 36864, F32)
                sinT = scr(36864, 40960, F32)
                ropeb = Buf("rope")
                S.op("sp", DMA(cosT, rope_d[:, 0:1024]), writes=[ropeb], dma_key="lp")
                S.op("sp", DMA(sinT, rope_d[:, 1024:2048]), writes=[ropeb], dma_key="lp")

                rt32 = [tmp(0, 2048, F32), tmp(2048, 4096, F32)]
                rta = [tmp(4096, 6144, F32), tmp(6144, 8192, F32)]
                rtb = [Buf("rt0"), Buf("rt1")]
                rstate = {"i": 0}

                def rope_evac(dst_ap, dst_buf, th, ps, pb):
                    i = rstate["i"]
                    rstate["i"] = (i + 1) % 2
                    t32, ta, tb_ = rt32[i], rta[i], rtb[i]
                    S.op("act", ACT(t32, ps[:, :], AF.Copy), reads=[pb], writes=[tb_])
                    ps2, pb2 = PS()
                    S.op("pe", MM(ps2[:, :], RmT32, t32, True, True), reads=[tb_, cbuf], writes=[pb2])
                    S.op("dve", TT(ta, t32, cosT[:, th * 512:(th + 1) * 512], ALU.mult), reads=[tb_, ropeb], writes=[tb_])
                    S.op("dve", TT(t32, ps2[:, :], sinT[:, th * 512:(th + 1) * 512], ALU.mult), reads=[pb2, ropeb, tb_], writes=[tb_])
                    S.op("dve", TT(dst_ap, ta, t32, ALU.add), reads=[tb_], writes=[dst_buf])

                chk("lp")
                kst = [scr(0, 4096).rearrange("p (a t) -> p a t", a=2), scr(4096, 8192).rearrange("p (a t) -> p a t", a=2)]
                kstb = [Buf("kst0"), Buf("kst1")]
                vst = [scr(8192, 12288).rearrange("p (a t) -> p a t", a=8), scr(12288, 16384).rearrange("p (a t) -> p a t", a=8)]
                vstb = [Buf("vst0"), Buf("vst1")]

                def gather_chunk(ci):
                    S.op("pool", (lambda ci: lambda e: e.collective_compute(
                        "AllGather", ALU.bypass, replica_groups=[[0, 1, 2, 3], [4, 5, 6, 7]],
                        ins=[kvloc[ci * 512:(ci + 1) * 512, :]], outs=[kvall[ci * 2048:(ci + 1) * 2048, :]]))(ci),
                        reads=[kvloc_b[ci]], writes=[kvall_b], dma_key="cc", dma_inc=1)

                for sec, (col0, roped) in enumerate(((OFF_FK, False), (OFF_DK, True))):
                    def evac_k(oc, th, ps, pb, sec=sec, roped=roped):
                        pn, j = divmod(oc, 2)
                        kb_, ks = kstb[pn % 2], kst[pn % 2]
                        dst = ks[:, j, th * 512:(th + 1) * 512]
                        if roped:
                            rope_evac(dst, kb_, th, ps, pb)
                        else:
                            S.op("act", ACT(dst, ps[:, :], AF.Copy), reads=[pb], writes=[kb_])
                        if j == 1 and th == 1:
                            r0 = sec * 1024 + pn * 256
                            S.op("sp", DMA(kvloc[r0:r0 + 256, :].rearrange("(j p) t -> p j t", p=128), ks[:, :, :]),
                                 reads=[kb_], writes=[kvloc_b[r0 // 512]], dma_key=f"kst{pn % 2}")
                            if pn % 2 == 1:
                                gather_chunk(r0 // 512)
                    proj_fm(w_in[L, :, col0:col0 + 1024], KC, x_rhs, evac_k)

                for sec, col0 in enumerate((OFF_FV, OFF_DV)):
                    for p in range(4):
                        slot, sb = load_panel(w_in[L, :, col0 + p * 256:col0 + (p + 1) * 256], KC)
                        vs, vb_ = vst[p % 2], vstb[p % 2]
                        for tb in range(8):
                            ps, pb = PS()
                            for kc in range(KC):
                                S.op("pe", MM(ps[:, 0:256], xhi[:, kc, tb * 128:(tb + 1) * 128], slot[:, kc, :], kc == 0, kc == KC - 1),
                                     reads=[sb, xbuf], writes=[pb])
                            S.op("act", ACT(vs[:, tb, :], ps[:, 0:256], AF.Copy), reads=[pb], writes=[vb_])
                        r0 = (2 + sec) * 1024
                        S.op("sp", DMA(kvloc[r0:r0 + 1024, p * 256:(p + 1) * 256].rearrange("(tb p) n -> p tb n", p=128), vs[:, :, :]),
                             reads=[vb_], writes=[kvloc_b[r0 // 512], kvloc_b[r0 // 512 + 1]], dma_key=f"vst{p % 2}")
                    gather_chunk(r0 // 512)
                    gather_chunk(r0 // 512 + 1)

                slot, sb = load_panel(w_in[L, :, OFF_FF:OFF_FF + 8], KC)
                lfst = small[:, 0:64].rearrange("p (a h) -> p a h", a=8)
                lfb = Buf("lfst")
                for tb in range(8):
                    ps, pb = PS()
                    for kc in range(KC):
                        S.op("pe", MM(ps[:, 0:8], xhi[:, kc, tb * 128:(tb + 1) * 128], slot[:, kc, 0:8], kc == 0, kc == KC - 1),
                             reads=[sb, xbuf], writes=[pb])
                    S.op("dve", TT(lfst[:, tb, :], ps[:, 0:8], bfbc[:, :], ALU.add), reads=[pb, lpb], writes=[lfb])
                S.op("act", ACT(small[:, 0:64], small[:, 0:64], AF.Exp, scale=-1.0), reads=[lfb], writes=[lfb])
                S.op("act", ACT(small[:, 0:64], small[:, 0:64], AF.Ln, bias=1.0), reads=[lfb], writes=[lfb])
                S.op("sp", DMA(lfloc.rearrange("(tb p) h -> p tb h", p=128), lfst), reads=[lfb], writes=[lfloc_b], dma_key="lfst")

                chk("p1")
                S.op("pool", lambda e: e.collective_compute("AllGather", ALU.bypass, replica_groups=[[0, 1, 2, 3], [4, 5, 6, 7]],
                                                            ins=[lfloc], outs=[lfall]),
                     reads=[lfloc_b], writes=[lfall_b], dma_key="cc", dma_inc=1)

                S.barrier_all(skip=("cc",))
                chk("cc")
                fq = scr(0, 16384).rearrange("p (h t) -> p h t", h=8)
                dq = scr(16384, 32768).rearrange("p (h t) -> p h t", h=8)
                ua = scr(32768, 49152).rearrange("p (h t) -> p h t", h=8)
                fqb = [[Buf(f"fq{h}_{t}") for t in range(2)] for h in range(8)]
                dqb = [[Buf(f"dq{h}_{t}") for t in range(2)] for h in range(8)]
                uab = [Buf(f"ua{c}") for c in range(8)]

                def evac_fq(oc, th, ps, pb):
                    S.op("act", ACT(fq[:, oc, th * 512:(th + 1) * 512], ps[:, :], AF.Copy), reads=[pb], writes=[fqb[oc][th]])
                proj_fm(w_in[L, :, OFF_FQ:OFF_FQ + 1024], KC, x_rhs, evac_fq)

                def evac_dq(oc, th, ps, pb):
                    rope_evac(dq[:, oc, th * 512:(th + 1) * 512], dqb[oc][th], th, ps, pb)
                proj_fm(w_in[L, :, OFF_DQ:OFF_DQ + 1024], KC, x_rhs, evac_dq)

                S.barrier_all(skip=("cc",))
                chk("p2a")
                def evac_u(oc, th, ps, pb):
                    S.op("act", ACT(ua[:, oc, th * 512:(th + 1) * 512], ps[:, :], AF.Gelu), reads=[pb], writes=[uab[oc]])
                proj_fm(w_in[L, :, OFF_U:OFF_U + 1024], KC, x_rhs, evac_u)

                chk("g1")
                glnG = tmp(0, 4096, F32)
                glnB = tmp(4096, 8192, F32)
                v32 = tmp(8192, 12288, F32)
                vsq = tmp(12288, 14336, F32)
                vn = [tmp(14336, 16384), tmp(16384, 18432)]
                wraw = tmp(18432, 20480, F32).rearrange("p (g s) -> p g s", g=4)
                bsbc = tmp(20480, 24576, F32).rearrange("p (c t) -> p c t", c=8)
                gpb = Buf("gparams")
                S.op("sp", DMA(glnG.unsqueeze(1), gln_g[L:L + 1, :].partition_broadcast(128)), writes=[gpb], dma_key="lp")
                S.op("sp", DMA(glnB.unsqueeze(1), gln_b[L:L + 1, :].partition_broadcast(128)), writes=[gpb], dma_key="lp")
                for g in range(4):
                    for rep in range(2):
                        S.op("sp", DMA(bsbc[:, 2 * g + rep, :].unsqueeze(1), gbs[L, g:g + 1, :].partition_broadcast(128)), writes=[gpb], dma_key="lp")
                S.op("sp", DMA(wraw, gws[L].rearrange("g t s -> t g s")), writes=[gpb], dma_key="lp")
                wmb = Buf("wm")
                for g in range(4):
                    ps, pb = PS()
                    S.op("pe", MM(ps[:, 0:128], wraw[:, g, :], ident32, True, True), reads=[gpb, cbuf], writes=[pb])
                    S.op("dve", TT(WmT[:, g, :], ps[:, 0:128], tri32, ALU.mult), reads=[pb, cbuf], writes=[wmb])

                chk("g2")
                vpan = scr(49152, 81920).rearrange("p (a c n) -> p a c n", a=4, c=16)
                vpb = Buf("vpan")
                for p in range(4):
                    S.op("pool", DMA(vpan[:, p, :, :], w_in[L, :, OFF_V + p * 256:OFF_V + (p + 1) * 256].rearrange("(c p) n -> p c n", p=128)),
                         writes=[vpb], dma_key="vpan")
                chk("g3")
                v32b, vsqb, stb = Buf("v32"), Buf("vsq"), Buf("gstat")
                vnb = [Buf("vn0"), Buf("vn1")]
                gs = small[:, 0:16]
                for tb in range(8):
                    for half in range(2):
                        ps, pb = PS()
                        for pp in range(2):
                            p = half * 2 + pp
                            for kc in range(KC):
                                S.op("pe", MM(ps[:, pp * 256:(pp + 1) * 256], xhi[:, kc, tb * 128:(tb + 1) * 128], vpan[:, p, kc, :], kc == 0, kc == KC - 1),
                                     reads=[vpb, xbuf], writes=[pb])
                        S.op("act", ACT(v32[:, half * 512:(half + 1) * 512], ps[:, :], AF.Gelu), reads=[pb], writes=[v32b])
                    if tb == 1:
                        chk("g7")
                    if tb == 0:
                        chk("g4")
                    S.op("dve", RSUM(gs[:, 0:1], v32), reads=[v32b], writes=[stb])
                    for half in range(2):
                        S.op("dve", TT(vsq, v32[:, half * 512:(half + 1) * 512], v32[:, half * 512:(half + 1) * 512], ALU.mult), reads=[v32b], writes=[vsqb])
                        S.op("dve", RSUM(gs[:, 1 + half:2 + half], vsq), reads=[vsqb], writes=[stb])
                    S.op("dve", TT(gs[:, 3:4], gs[:, 1:2], gs[:, 2:3], ALU.add), reads=[stb], writes=[stb])
                    S.op("dve", TS(gs[:, 4:5], gs[:, 0:1], 1.0 / 1024, None, ALU.mult), reads=[stb], writes=[stb])
                    S.op("dve", TT(gs[:, 5:6], gs[:, 4:5], gs[:, 4:5], ALU.mult), reads=[stb], writes=[stb])
                    S.op("dve", STT(gs[:, 6:7], gs[:, 3:4], 1.0 / 1024, gs[:, 5:6], ALU.mult, ALU.subtract), reads=[stb], writes=[stb])
                    S.op("dve", TS(gs[:, 7:8], gs[:, 6:7], 1e-5, None, ALU.add), reads=[stb], writes=[stb])
                    S.op("act", ACT(gs[:, 7:8], gs[:, 7:8], AF.Ln), reads=[stb], writes=[stb])
                    S.op("act", ACT(gs[:, 7:8], gs[:, 7:8], AF.Exp, scale=-0.5), reads=[stb], writes=[stb])
                    S.op("dve", TS(v32, v32, gs[:, 4:5], gs[:, 7:8], ALU.subtract, ALU.mult), reads=[stb, v32b], writes=[v32b])
                    S.op("dve", TT(v32, v32, glnG, ALU.mult), reads=[gpb, v32b], writes=[v32b])
                    if tb == 0:
                        chk("g5")
                    vnt, vntb = vn[tb % 2], vnb[tb % 2]
                    S.op("dve", TT(vnt, v32, glnB, ALU.add), reads=[gpb, v32b], writes=[vntb])
                    for bank in range(2):
                        ps, pb = PS()
                        for cc4 in range(4):
                            cc = bank * 4 + cc4
                            S.op("pe", MM(ps[:, cc4 * 128:(cc4 + 1) * 128], vnt[:, cc * 128:(cc + 1) * 128], WmT[:, cc // 2, :], True, True),
                                 reads=[vntb, wmb], writes=[pb])
                        if tb == 0 and bank == 0:
                            chk("g6")
                        for cc4 in range(4):
                            cc = bank * 4 + cc4
                            S.op("dve", TT(vsq[:, 0:128], ps[:, cc4 * 128:(cc4 + 1) * 128], bsbc[:, cc, :], ALU.add), reads=[pb, gpb], writes=[vsqb])
                            S.op("dve", TT(ua[:, cc, tb * 128:(tb + 1) * 128], vsq[:, 0:128], ua[:, cc, tb * 128:(tb + 1) * 128], ALU.mult),
                                 reads=[vsqb, uab[cc]], writes=[uab[cc]])

                S.barrier_all()
                chk("p2b")
                KT = scr(49152, 65536).rearrange("p (m r t) -> p m r t", m=2, r=4)
                Vb = scr(65536, 82048).rearrange("p (k c) -> p k c", c=258)
                ktb = [Buf("kt0"), Buf("kt1")]
                vbb = Buf("vb")
                PT = [tmp(0, 1024), tmp(1024, 2048), tmp(2048, 3072)]
                ptb = [Buf("pt0"), Buf("pt1"), Buf("pt2")]
                otmp = tmp(3072, 7168, F32).rearrange("p (j c) -> p j c", j=4)
                ores = tmp(7168, 11264, F32).rearrange("p (j c) -> p j c", j=4)
                obf = tmp(11264, 13312).rearrange("p (j c) -> p j c", j=4)
                biasT = tmp(13312, 21504, F32).rearrange("p (k m h) -> p k m h", k=32, m=8)
                lftm = tmp(21504, 22528, F32)
                otb, orb, obb, bib, lfb2 = Buf("otmp"), Buf("ores"), Buf("obf"), Buf("bias"), Buf("lftm")
                osm = small[:, 16:48]
                osb = Buf("osm")

                lftm4 = lftm.rearrange("p (g i h) -> p g i h", g=8, i=4)
                for i_ in range(4):
                    S.op("sp", DMA(lftm4[:, :, i_, :], lfall[i_ * 1024:(i_ + 1) * 1024, :].rearrange("(g p) h -> p g h", p=128)),
                         reads=[lfall_b], writes=[lfb2], dma_key="lp")
                ps, pb = PS()
                S.op("pe", MM(ps[:, 0:256], tri32, lftm, True, True), reads=[lfb2, cbuf], writes=[pb])
                ps2, pb2 = PS()
                S.op("pe", MM(ps2[:, 0:256], ones32, lftm, True, True), reads=[lfb2, cbuf], writes=[pb2])
                ceb = Buf("ce")
                S.op("dve", MSET(Ee[:, 0, :], 0.0), writes=[ceb])
                for kb in range(32):
                    S.op("dve", TT(Ee[:, kb + 1, :], Ee[:, kb, :], ps2[:, kb * 8:(kb + 1) * 8], ALU.add), reads=[pb2, ceb], writes=[ceb])
                S.op("dve", TT(Cc[:, :, :], ps[:, 0:256].rearrange("p (k h) -> p k h", k=32), Ee[:, 0:32, :], ALU.add), reads=[pb, ceb], writes=[ceb])
                Ee4 = Ee[:, 0:32, :].rearrange("p (m i) h -> p m i h", i=4)
                S.op("dve", TS(refE[:, :, :], Ee4[:, :, 0, :], selt[:, 0:1], None, ALU.mult), reads=[ceb, cbuf], writes=[bib])
                for i_ in range(1, 4):
                    S.op("dve", STT(refE[:, :, :], Ee4[:, :, i_, :], selt[:, i_:i_ + 1], refE[:, :, :], ALU.mult, ALU.add), reads=[ceb, cbuf, bib], writes=[bib])
                for m in range(8):
                    S.op("dve", TT(biasT[:, :, m, :], Cc[:, :, :], refE[:, m:m + 1, :].broadcast_to([128, 32, 8]), ALU.subtract),
                         reads=[ceb, bib], writes=[bib])
                    S.op("dve", TT(biasT[:, 4 * m:4 * m + 4, m, :], biasT[:, 4 * m:4 * m + 4, m, :],
                                   selt[:, 4:8].unsqueeze(2).broadcast_to([128, 4, 8]), ALU.add), reads=[bib, cbuf], writes=[bib])
                chk("a1")
                pt_state = {"i": 0, "o": 0, "s": 0}

                def attn_unit(kind, u):
                    if kind == "fox":
                        krow0, vrow0, vcol0 = 0, 2048, u * 256
                        qt, qtb = fq, fqb
                    else:
                        krow0, vrow0, vcol0 = 1024, 3072, u * 256
                        qt, qtb = dq, dqb
                    for mp in range(2):
                        hm = 2 * u + mp
                        ci = krow0 // 512 + hm // 4
                        src = kvall[ci * 2048:(ci + 1) * 2048, :].rearrange("(r q) t -> q r t", r=4)[(hm % 4) * 128:(hm % 4 + 1) * 128, :, :]
                        S.op("sp", DMA(KT[:, mp, :, :], src), reads=[kvall_b], writes=[ktb[mp]], dma_key=f"kt{mp}")
                    for r_ in range(4):
                        for half in range(2):
                            ci = vrow0 // 512 + half
                            rows = kvall[ci * 2048 + r_ * 512:ci * 2048 + (r_ + 1) * 512, :]
                            k0 = r_ * 8 + half * 4
                            if kind == "fox":
                                for hh in range(2):
                                    S.op("sp", DMA(Vb[:, k0:k0 + 4, hh * 129:hh * 129 + 128],
                                                   rows[:, vcol0 + hh * 128:vcol0 + (hh + 1) * 128].rearrange("(g p) n -> p g n", p=128)),
                                         reads=[kvall_b], writes=[vbb], dma_key="vb")
                            else:
                                S.op("sp", DMA(Vb[:, k0:k0 + 4, 0:256], rows[:, vcol0:vcol0 + 256].rearrange("(g p) n -> p g n", p=128)),
                                     reads=[kvall_b], writes=[vbb], dma_key="vb")
                    if kind == "fox":
                        S.op("pool", MSET(Vb[:, :, 128:129], 1.0), writes=[vbb])
                        S.op("pool", MSET(Vb[:, :, 257:258], 1.0), writes=[vbb])
                    else:
                        S.op("pool", MSET(Vb[:, :, 256:257], 1.0), writes=[vbb])

                    for m0 in (0, 4):
                        tq = m0 // 4
                        for mp in range(2):
                            hm = 2 * u + mp
                            dvw = 129 if kind == "fox" else 257
                            vc0 = mp * 129 if kind == "fox" else 0
                            if kind == "fox":
                                O = [(psb[b_][:, 0:129], ps_bufs[b_]) for b_ in range(4)]
                            else:
                                O = [(psb[b_][:, 0:257], ps_bufs[b_]) for b_ in range(4)]
                            first = [True] * 4
                            ngroups = m0 + 4
                            def stage1(g, i):
                                j0 = max(0, g - m0)
                                c0 = j0 * 128
                                kbi = 4 * g + i
                                sb_ = 4 + pt_state["s"] % 3
                                pt_state["s"] += 1
                                ps, pb = psb[sb_], ps_bufs[sb_]
                                S.op("pe", MM(ps[:, c0:512], KT[:, mp, i, g * 128:(g + 1) * 128], qt[:, hm, m0 * 128 + c0:(m0 + 4) * 128], True, True),
                                     reads=[ktb[mp], qtb[hm][tq]], writes=[pb])
                                pi = pt_state["i"]
                                pt_state["i"] = (pi + 1) % 3
                                pt, ptbuf = PT[pi], ptb[pi]
                                if kind == "fox":
                                    for j in range(j0, 4):
                                        S.op("act", ACT(pt[:, j * 128:(j + 1) * 128], ps[:, j * 128:(j + 1) * 128], AF.Exp,
                                                        bias=biasT[:, kbi, m0 + j, hm:hm + 1], scale=SCALE),
                                             reads=[pb, bib], writes=[ptbuf])
                                else:
                                    S.op("act", ACT(pt[:, c0:512], ps[:, c0:512], AF.Exp, scale=SCALE), reads=[pb], writes=[ptbuf])
                                if g >= m0:
                                    S.op("pool", TT(pt[:, c0:c0 + 128], pt[:, c0:c0 + 128], dmask[:, i, :], ALU.mult), reads=[ptbuf, cbuf], writes=[ptbuf])
                                return (g, i, j0, kbi, pt, ptbuf)

                            def stage2(tup):
                                g, i, j0, kbi, pt, ptbuf = tup
                                for j in range(j0, 4):
                                    is_last = (4 * (m0 + j) + 3 == kbi)
                                    S.op("pe", MM(O[j][0], pt[:, j * 128:(j + 1) * 128], Vb[:, i * 8 + g, vc0:vc0 + dvw], first[j], is_last),
                                         reads=[ptbuf, vbb], writes=[O[j][1]])
                                    first[j] = False

                            pend = []
                            for g in range(ngroups):
                                for i in range(4):
                                    pend.append(stage1(g, i))
                                    if len(pend) > 2:
                                        stage2(pend.pop(0))
                            while pend:
                                stage2(pend.pop(0))
                            for j in range(4):
                                oj, ojb = O[j]
                                tok0 = (m0 + j) * 128
                                if kind == "fox":
                                    S.op("dve", RCP(osm[:, j:j + 1], oj[:, 128:129]), reads=[ojb], writes=[osb])
                                    S.op("dve", TS(obf[:, j, 0:128], oj[:, 0:128], osm[:, j:j + 1], None, ALU.mult), reads=[ojb, osb], writes=[obb])
                                    S.op("pe", TR(pst[:, j * 128:(j + 1) * 128], obf[:, j, 0:128], ident16), reads=[obb, cbuf], writes=[pst_buf])
                                    S.op("act", ACT(fq[:, hm, tok0:tok0 + 128], pst[:, j * 128:(j + 1) * 128], AF.Copy), reads=[pst_buf], writes=[fqb[hm][tq]])
                                elif mp == 0:
                                    S.op("dve", RCP(osm[:, j:j + 1], oj[:, 256:257]), reads=[ojb], writes=[osb])
                                    S.op("dve", TS(otmp[:, j, :], oj[:, 0:256], osm[:, j:j + 1], None, ALU.mult), reads=[ojb, osb], writes=[otb])
                                else:
                                    S.op("dve", RCP(osm[:, 8 + j:9 + j], oj[:, 256:257]), reads=[ojb], writes=[osb])
                                    S.op("dve", TT(osm[:, 8 + j:9 + j], osm[:, 8 + j:9 + j], neg_lam, ALU.mult), reads=[osb, lpb], writes=[osb])
                                    S.op("dve", STT(ores[:, j, :], oj[:, 0:256], osm[:, 8 + j:9 + j], otmp[:, j, :], ALU.mult, ALU.add),
                                         reads=[ojb, osb, otb], writes=[orb])
                                    S.op("dve", TT(otmp[:, j, :], ores[:, j, :], ores[:, j, :], ALU.mult), reads=[orb], writes=[otb])
                                    S.op("dve", RSUM(osm[:, 16 + j:17 + j], otmp[:, j, :]), reads=[otb], writes=[osb])
                                    S.op("dve", TS(osm[:, 16 + j:17 + j], osm[:, 16 + j:17 + j], 1.0 / 256, 1e-5, ALU.mult, ALU.add), reads=[osb], writes=[osb])
                                    S.op("act", ACT(osm[:, 16 + j:17 + j], osm[:, 16 + j:17 + j], AF.Ln), reads=[osb], writes=[osb])
                                    S.op("act", ACT(osm[:, 16 + j:17 + j], osm[:, 16 + j:17 + j], AF.Exp, scale=-0.5), reads=[osb], writes=[osb])
                                    S.op("dve", STT(obf[:, j, :], ores[:, j, :], osm[:, 16 + j:17 + j], dngbc[:, :], ALU.mult, ALU.mult),
                                         reads=[orb, osb, lpb], writes=[obb])
                                    for c in range(2):
                                        S.op("pe", TR(pst[:, (2 * j + c) * 128:(2 * j + c + 1) * 128], obf[:, j, c * 128:(c + 1) * 128], ident16),
                                             reads=[obb, cbuf], writes=[pst_buf])
                                    for c in range(2):
                                        S.op("act", ACT(dq[:, 2 * u + c, tok0:tok0 + 128], pst[:, (2 * j + c) * 128:(2 * j + c + 1) * 128], AF.Copy),
                                             reads=[pst_buf], writes=[dqb[2 * u + c][tq]])

                for u in range(4):
                    attn_unit("fox", u)
                    if u == 0:
                        chk("a2")
                chk("a3")
                for u in range(4):
                    attn_unit("diff", u)

                S.barrier_all()
                if debug:
                    dt_ = [tmp(0, 2048, F32), tmp(2048, 4096, F32)]
                    db_ = [Buf("dbg0"), Buf("dbg1")]
                    n_ = 0
                    for si_, (src_, bufs_) in enumerate(((ua, None), (fq, fqb), (dq, dqb))):
                        for c_ in range(8):
                            for th_ in range(2):
                                rb_ = uab[c_] if bufs_ is None else bufs_[c_][th_]
                                S.op("act", ACT(dt_[n_ % 2], src_[:, c_, th_ * 512:(th_ + 1) * 512], AF.Copy), reads=[rb_], writes=[db_[n_ % 2]])
                                S.op("sp", DMA(dbg[si_ * 1024 + c_ * 128:si_ * 1024 + (c_ + 1) * 128, th_ * 512:(th_ + 1) * 512], dt_[n_ % 2]),
                                     reads=[db_[n_ % 2]], writes=[outb], dma_key=f"dbg{n_ % 2}")
                                n_ += 1
                chk("p3")
                merged = scr(49152, 81920).rearrange("p (c t) -> p c t", c=16)
                mgb = [Buf(f"mg{c}") for c in range(16)]
                gsig = [tmp(0, 4096).rearrange("p (a t) -> p a t", a=2), tmp(4096, 8192).rearrange("p (a t) -> p a t", a=2)]
                gsb = [Buf("gs0"), Buf("gs1")]
                macc = tmp(8192, 16384, F32).rearrange("p (a t) -> p a t", a=2)
                mab = [[Buf(f"ma{a}{t}") for t in range(2)] for a in range(2)]
                tmpm = [tmp(16384, 18432, F32), tmp(18432, 20480, F32)]
                tmb = [Buf("tm0"), Buf("tm1")]
                tstate = {"i": 0}
                for ocp in range(8):
                    for bi in range(3):
                        gs_t, gs_b = gsig[bi % 2], gsb[bi % 2]

                        def evac_gate(oc, th, ps, pb, bi=bi, gs_t=gs_t, gs_b=gs_b, ocp=ocp):
                            col = pvo + bi * 16 + ocp * 2 + oc
                            S.op("act", ACT(gs_t[:, oc, th * 512:(th + 1) * 512], ps[:, :], AF.Sigmoid, bias=pv[:, col:col + 1]),
                                 reads=[pb, cbuf], writes=[gs_b])
                        proj_fm(w_gate[L, :, bi * 2048 + ocp * 256:bi * 2048 + (ocp + 1) * 256], KC, x_rhs, evac_gate)

                        src_t = (ua, fq, dq)[bi]

                        def br_rhs(kc, th, bi=bi, src_t=src_t):
                            if bi == 0:
                                b = uab[kc]
                            elif bi == 1:
                                b = fqb[kc][th]
                            else:
                                b = dqb[kc][th]
                            return src_t[:, kc, th * 512:(th + 1) * 512], b

                        def evac_br(oc, th, ps, pb, bi=bi, gs_t=gs_t, gs_b=gs_b, ocp=ocp):
                            g_ap = gs_t[:, oc, th * 512:(th + 1) * 512]
                            acc = macc[:, oc, th * 512:(th + 1) * 512]
                            ab = mab[oc][th]
                            if bi == 0:
                                S.op("dve", TT(acc, ps[:, :], g_ap, ALU.mult), reads=[pb, gs_b], writes=[ab])
                            else:
                                ti = tstate["i"]
                                tstate["i"] = (ti + 1) % 2
                                S.op("dve", TT(tmpm[ti], ps[:, :], g_ap, ALU.mult), reads=[pb, gs_b], writes=[tmb[ti]])
                                if bi == 1:
                                    S.op("pool", TT(acc, acc, tmpm[ti], ALU.add), reads=[tmb[ti], ab], writes=[ab])
                                else:
                                    S.op("pool", TT(merged[:, ocp * 2 + oc, th * 512:(th + 1) * 512], acc, tmpm[ti], ALU.add),
                                         reads=[tmb[ti], ab], writes=[mgb[ocp * 2 + oc]])
                        proj_fm(w_br[bi][L, :, ocp * 256:(ocp + 1) * 256], 8, br_rhs, evac_br)

                S.barrier_all()

                chk("p4")
                mean_t = tmp(0, 2048, F32)
                rstd_t = tmp(2048, 4096, F32)
                sqt = [tmp(4096, 6144, F32), tmp(6144, 8192, F32)]
                y32 = [tmp(8192, 10240, F32), tmp(10240, 12288, F32)]
                sqb = [Buf("sq0"), Buf("sq1")]
                y32b = [Buf("y0"), Buf("y1")]
                stat_b = Buf("lnstat")

                def layer_norm(pre, preb, th, gcol, bcol, final):
                    ps_s, pb_s = PS()
                    ps_q, pb_q = PS()
                    for c in range(KC):
                        S.op("pe", MM(ps_s[:, :], ones32, pre[:, c, :], c == 0, c == KC - 1), reads=[preb, cbuf], writes=[pb_s])
                    for c in range(KC):
                        S.op("act", ACT(sqt[c % 2], pre[:, c, :], AF.Square), reads=[preb], writes=[sqb[c % 2]])
                        S.op("pe", MM(ps_q[:, :], ones32, sqt[c % 2], c == 0, c == KC - 1), reads=[sqb[c % 2], cbuf], writes=[pb_q])
                    S.op("dve", TS(mean_t, ps_s[:, :], 1.0 / D, None, ALU.mult), reads=[pb_s], writes=[stat_b])
                    S.op("dve", TT(rstd_t, mean_t, mean_t, ALU.mult), reads=[stat_b], writes=[stat_b])
                    S.op("dve", STT(rstd_t, ps_q[:, :], 1.0 / D, rstd_t, ALU.mult, ALU.subtract), reads=[pb_q, stat_b], writes=[stat_b])
                    S.op("dve", TS(rstd_t, rstd_t, 1e-5, None, ALU.add), reads=[stat_b], writes=[stat_b])
                    S.op("act", ACT(rstd_t, rstd_t, AF.Ln), reads=[stat_b], writes=[stat_b])
                    S.op("act", ACT(rstd_t, rstd_t, AF.Exp, scale=-0.5), reads=[stat_b], writes=[stat_b])
                    for c in range(KC):
                        y, yb = y32[c % 2], y32b[c % 2]
                        S.op("dve", TT(y, pre[:, c, :], mean_t, ALU.subtract), reads=[preb, stat_b], writes=[yb])
                        S.op("dve", TT(y, y, rstd_t, ALU.mult), reads=[stat_b, yb], writes=[yb])
                        S.op("dve", TS(y, y, pv[:, gcol + c:gcol + c + 1], pv[:, bcol + c:bcol + c + 1], ALU.mult, ALU.add), reads=[cbuf, yb], writes=[yb])
                        if final:
                            S.op("sp", DMA(outT[c * 128:(c + 1) * 128, th * 512:(th + 1) * 512], y), reads=[yb], writes=[outb], dma_key=f"out{c % 2}")
                        else:
                            hi = xhi[:, c, th * 512:(th + 1) * 512]
                            S.op("act", ACT(hi, y, AF.Copy), reads=[yb], writes=[xbuf])
                            S.op("pool", TT(xlo[:, c, th * 512:(th + 1) * 512], y, hi, ALU.subtract), reads=[yb, xbuf], writes=[xbuf])

                preh = scr(0, 32768, F32).rearrange("p (c t) -> p c t", c=16)
                for th in range(2):
                    prehb = Buf(f"preh{th}")

                    def mg_rhs(kc, th_, th=th):
                        return merged[:, kc, th * 512:(th + 1) * 512], mgb[kc]

                    def evac_mix(oc, th_, ps, pb, th=th, prehb=prehb):
                        ti = tstate["i"]
                        tstate["i"] = (ti + 1) % 2
                        t_, tb_ = tmpm[ti], tmb[ti]
                        S.op("pool", TT(t_, xhi[:, oc, th * 512:(th + 1) * 512], xlo[:, oc, th * 512:(th + 1) * 512], ALU.add), reads=[xbuf], writes=[tb_])
                        S.op("dve", STT(preh[:, oc, :], t_, ALPHA, ps[:, :], ALU.mult, ALU.add), reads=[tb_, pb], writes=[prehb])
                    proj_fm(w_out[L, :, :], KC, mg_rhs, evac_mix, th_list=(th,))
                    layer_norm(preh, prehb, th, pvo + 48, pvo + 64, False)
                    S.barrier_all()

                chk("p5")
                pre = scr(0, 65536, F32).rearrange("p (c t) -> p c t", c=16)
                hT = scr(65536, 81920).rearrange("p (c t) -> p c t", c=8)
                preb = [Buf("pre0"), Buf("pre1")]
                hb = [Buf(f"h{c}") for c in range(8)]
                for q in range(8):
                    def evac_up(oc, th, ps, pb, q=q):
                        ti = tstate["i"]
                        tstate["i"] = (ti + 1) % 2
                        S.op("act", ACT(tmpm[ti], ps[:, :], AF.Relu), reads=[pb], writes=[tmb[ti]])
                        S.op("dve", TT(hT[:, oc, th * 512:(th + 1) * 512], tmpm[ti], tmpm[ti], ALU.mult), reads=[tmb[ti]], writes=[hb[oc]])
                    proj_fm(w_up[L, :, q * 1024:(q + 1) * 1024], KC, x_rhs, evac_up)

                    def h_rhs(kc, th):
                        return hT[:, kc, th * 512:(th + 1) * 512], hb[kc]

                    def evac_dn(oc, th, ps, pb, q=q):
                        dst = pre[:, oc, th * 512:(th + 1) * 512]
                        if q == 0:
                            ti = tstate["i"]
                            tstate["i"] = (ti + 1) % 2
                            t_, tb_ = tmpm[ti], tmb[ti]
                            S.op("pool", TT(t_, xhi[:, oc, th * 512:(th + 1) * 512], xlo[:, oc, th * 512:(th + 1) * 512], ALU.add), reads=[xbuf], writes=[tb_])
                            S.op("dve", STT(dst, t_, ALPHA, ps[:, :], ALU.mult, ALU.add), reads=[tb_, pb], writes=[preb[th]])
                        else:
                            S.op("dve", TT(dst, dst, ps[:, :], ALU.add), reads=[pb, preb[th]], writes=[preb[th]])
                    proj_fm(w_down[L, q * 1024:(q + 1) * 1024, :], 8, h_rhs, evac_dn)
                S.barrier_all()
                for th in range(2):
                    layer_norm(pre[:, :, th * 512:(th + 1) * 512], preb[th], th, pvo + 80, pvo + 96, L == depth - 1)

        except _Stop:
            pass
        S.barrier_all()
        S.emit(st)
        nops = sum(len(v) for v in S.ops.values())
        print(f"[kernel] ops={nops} sems={S.n_sems}", flush=True)
    return nc


def _consts():
    ident = np.eye(128, dtype=np.float32)
    p = np.arange(128)[:, None]
    c = np.arange(128)[None, :]
    tri = (c >= p).astype(np.float32)
    ones = np.ones((128, 128), np.float32)
    rmt = np.zeros((128, 128), np.float32)
    for d in range(64):
        rmt[d + 64, d] = -1.0
        rmt[d, d + 64] = 1.0
    return np.concatenate([ident, tri, ones, rmt], axis=1)


def _dmask(r):
    p = np.arange(128)[:, None]
    c = np.arange(128)[None, :]
    tri = (c >= p).astype(np.float32)
    out = np.zeros((128, 4, 128), np.float32)
    for i in range(4):
        if i < r:
            out[:, i, :] = 1.0
        elif i == r:
            out[:, i, :] = tri
    return out.reshape(128, 512)


def _rope(r):
    pos = np.concatenate([(r + 4 * m) * 128 + np.arange(128) for m in range(8)]).astype(np.float32)
    inv_freq = (10000.0 ** (-np.arange(0, 128, 2, dtype=np.float32) / 128)).astype(np.float32)
    ang = pos[None, :] * inv_freq[:, None]
    cos = np.cos(ang).astype(np.float32)
    sin = np.sin(ang).astype(np.float32)
    return np.concatenate([np.concatenate([cos, cos], 0), np.concatenate([sin, sin], 0)], axis=1)


_NC_CACHE = {}


def kernel(x, w_in, b_forget, gmlp_ln_g, gmlp_ln_b, gmlp_w_s, gmlp_b_s, lam_q1, lam_k1, lam_q2, lam_k2, diff_norm_g,
           w_branch_a, w_branch_b, w_branch_c, w_gate, b_gate, w_out, ln_mix_g, ln_mix_b, w_up, w_down, ln_mlp_g, ln_mlp_b,
           _depth=DEPTH, _debug=False, _stop=None):
    f = lambda a: np.ascontiguousarray(np.asarray(a, dtype=np.float32))
    x = f(x)
    key = (_depth, _debug, _stop)
    if key not in _NC_CACHE:
        _NC_CACHE[key] = build(_depth, _debug, _stop)
    nc = _NC_CACHE[key]

    def fm(v):
        v = f(v)
        return v.reshape(DEPTH, -1, 128).transpose(2, 0, 1)
    pvec = np.concatenate([fm(b_gate), fm(ln_mix_g), fm(ln_mix_b), fm(ln_mlp_g), fm(ln_mlp_b)], axis=2)
    pvec = np.ascontiguousarray(pvec.reshape(128, DEPTH * NPV))
    lamv = np.ascontiguousarray(np.concatenate([f(lam_q1), f(lam_k1), f(lam_q2), f(lam_k2)], axis=1))
    shared = {
        "w_in": f(w_in[:_depth]), "w_gate": f(w_gate[:_depth]), "w_br0": f(w_branch_a[:_depth]), "w_br1": f(w_branch_b[:_depth]),
        "w_br2": f(w_branch_c[:_depth]), "w_out": f(w_out[:_depth]), "w_up": f(w_up[:_depth]), "w_down": f(w_down[:_depth]), "pvec": pvec, "cst": _consts(),
        "b_forget": f(b_forget), "gln_g": f(gmlp_ln_g), "gln_b": f(gmlp_ln_b), "gws": f(gmlp_w_s), "gbs": f(gmlp_b_s),
        "lamv": lamv, "dng": f(diff_norm_g),
    }
    in_maps = []
    for c in range(8):
        b, r = divmod(c, 4)
        blocks = [x[b, (r + 4 * m) * 128:(r + 4 * m + 1) * 128, :] for m in range(8)]
        xs = np.concatenate(blocks, axis=0)
        m = dict(shared)
        m["xT"] = np.ascontiguousarray(xs.T)
        m["dmask"] = _dmask(r)
        m["rope"] = _rope(r)
        sel = np.zeros((128, 8), np.float32)
        sel[:, r] = 1.0
        sel[:, 4 + r + 1:8] = -30000.0
        m["sel"] = sel
        in_maps.append(m)
    res = run_bass_kernel_spmd(nc, in_maps, core_ids=list(range(8)))
    out = np.empty((2, 4096, 2048), np.float32)
    for c in range(8):
        b, r = divmod(c, 4)
        o = res.results[c]["outT"].T
        for m in range(8):
            out[b, (r + 4 * m) * 128:(r + 4 * m + 1) * 128, :] = o[m * 128:(m + 1) * 128, :]
    if _debug:
        return out, res
    return out
```

```python
import math
from contextlib import ExitStack

import numpy as np
import concourse.bass as bass
import concourse.mybir as mybir
from concourse.bass_utils import run_bass_kernel_spmd

F32 = mybir.dt.float32
BF16 = mybir.dt.bfloat16
AF = mybir.ActivationFunctionType
ALU = mybir.AluOpType
AX = mybir.AxisListType

ENGS = ("pe", "act", "dve", "pool", "sp")
SEM_CHUNK = 4000
DEPTH = 4
D = 2048
KC = 16
TOK = 1024
OFF_U, OFF_V, OFF_FQ, OFF_FK, OFF_FV, OFF_FF, OFF_DQ, OFF_DK, OFF_DV = 0, 1024, 2048, 3072, 4096, 5120, 5128, 6152, 7176
ALPHA = (2 * DEPTH) ** 0.25
SCALE = 128 ** -0.5
NPV = 112


class Buf:
    __slots__ = ("name", "writers", "readers")

    def __init__(self, name=""):
        self.name = name
        self.writers = []
        self.readers = []


class Op:
    __slots__ = ("eng", "fn", "deps", "signal", "idx", "dma_key", "sig_no", "dma_cnt")

    def __init__(self, eng, fn):
        self.eng = eng
        self.fn = fn
        self.deps = {}
        self.signal = False
        self.idx = None
        self.dma_key = None
        self.sig_no = None
        self.dma_cnt = 0


class Sched:
    def __init__(self, nc):
        self.nc = nc
        self.ops = {e: [] for e in ENGS}
        self.dma_counts = {}
        self.dma_inc = {}

    def _add_dep(self, op, prod):
        if prod is None or prod is op:
            return
        if prod.dma_key is not None:
            k = ("d", prod.dma_key)
            v = self.dma_counts[prod.dma_key]
            if op.deps.get(k, 0) < v:
                op.deps[k] = v
        else:
            if prod.eng == "pe" and op.eng == "pe":
                return
            k = ("e", prod.eng)
            cur = op.deps.get(k)
            if cur is None or cur.idx < prod.idx:
                op.deps[k] = prod

    def op(self, eng, fn, reads=(), writes=(), dma_key=None, dma_inc=16):
        o = Op(eng, fn)
        o.idx = len(self.ops[eng])
        for b in reads:
            for w in b.writers:
                self._add_dep(o, w)
        for b in writes:
            for w in b.writers:
                self._add_dep(o, w)
            for r in b.readers:
                self._add_dep(o, r)
        if dma_key is not None:
            o.dma_key = dma_key
            self.dma_inc[dma_key] = dma_inc
            self.dma_counts[dma_key] = self.dma_counts.get(dma_key, 0) + 1
            o.dma_cnt = self.dma_counts[dma_key]
        for b in reads:
            b.readers.append(o)
        for b in writes:
            if b.readers:
                b.writers = [o]
                b.readers = []
            elif not b.writers or b.writers[-1] is not o:
                kk = o.dma_key if o.dma_key is not None else o.eng
                b.writers = [w for w in b.writers if (w.dma_key if w.dma_key is not None else w.eng) != kk]
                b.writers.append(o)
        self.ops[eng].append(o)
        return o

    def barrier_all(self, skip=()):
        last = {}
        for e in ENGS:
            last[e] = None
            for o in reversed(self.ops[e]):
                if o.fn is not None and o.dma_key is None:
                    last[e] = o
                    break
        dk = dict(self.dma_counts)
        for e in ENGS:
            o = Op(e, None)
            o.idx = len(self.ops[e])
            for e2 in ENGS:
                p = last[e2]
                if p is None or (e == "pe" and e2 == "pe"):
                    continue
                o.deps[("e", e2)] = p
            for k, v in dk.items():
                if k not in skip:
                    o.deps[("d", k)] = v
            self.ops[e].append(o)

    def emit(self, stack):
        nc = self.nc
        for e in ENGS:
            for o in self.ops[e]:
                for k, v in o.deps.items():
                    if k[0] == "e":
                        v.signal = True
        esems = {}
        for e in ENGS:
            n = 0
            for o in self.ops[e]:
                if o.dma_key is None and o.signal:
                    o.sig_no = n
                    n += 1
            nsem = (n + SEM_CHUNK - 1) // SEM_CHUNK
            esems[e] = [stack.enter_context(nc.semaphore(f"s_{e}{i}")) for i in range(nsem)]
        per = {k: SEM_CHUNK // self.dma_inc[k] for k in self.dma_counts}
        dsems = {k: [stack.enter_context(nc.semaphore(f"d_{k}_{i}")) for i in range((n - 1) // per[k] + 1)]
                 for k, n in self.dma_counts.items()}
        self.n_sems = sum(len(v) for v in esems.values()) + sum(len(v) for v in dsems.values())
        block = stack.enter_context(nc.Block())

        def run(e, engobj):
            waited = {}
            for o in self.ops[e]:
                for k, v in o.deps.items():
                    if k[0] == "e":
                        c, r = divmod(v.sig_no, SEM_CHUNK)
                        todo = [(("e", k[1], c), r + 1, esems[k[1]][c])]
                    else:
                        key = k[1]
                        p_, inc = per[key], self.dma_inc[key]
                        c_last = (v - 1) // p_
                        todo = [(("d", key, c), (p_ if c < c_last else v - c_last * p_) * inc, dsems[key][c]) for c in range(c_last + 1)]
                    for sk, val, sem in todo:
                        if waited.get(sk, 0) >= val:
                            continue
                        waited[sk] = val
                        engobj.wait_ge(sem, val)
                if o.fn is None:
                    continue
                ins = o.fn(engobj)
                if o.dma_key is not None:
                    ins.then_inc(dsems[o.dma_key][(o.dma_cnt - 1) // per[o.dma_key]], self.dma_inc[o.dma_key])
                elif o.signal:
                    c, r = divmod(o.sig_no, SEM_CHUNK)
                    ins.then_inc(esems[e][c], 1)

        block.tensor(lambda eng: run("pe", eng))
        block.scalar(lambda eng: run("act", eng))
        block.vector(lambda eng: run("dve", eng))
        block.gpsimd(lambda eng: run("pool", eng))
        block.sync(lambda eng: run("sp", eng))


def MM(out, lhsT, rhs, start, stop):
    return lambda e: e.matmul(out, lhsT, rhs, start=start, stop=stop)


def TR(out, in_, ident):
    return lambda e: e.transpose(out, in_, ident)


def ACT(out, in_, func, bias=None, scale=None):
    kw = {}
    if bias is not None:
        kw["bias"] = bias
    if scale is not None:
        kw["scale"] = scale
    return lambda e: e.activation(out=out, in_=in_, func=func, **kw)


def TT(out, a, b, op):
    return lambda e: e.tensor_tensor(out, a, b, op)


def TS(out, a, s1, s2, op0, op1=None):
    if op1 is None:
        return lambda e: e.tensor_scalar(out, a, s1, None, op0)
    return lambda e: e.tensor_scalar(out, a, s1, s2, op0, op1)


def STT(out, a, s, b, op0, op1):
    return lambda e: e.scalar_tensor_tensor(out, a, s, b, op0, op1)


def CP(out, in_):
    return lambda e: e.tensor_copy(out, in_)


def RSUM(out, in_):
    return lambda e: e.reduce_sum(out, in_, AX.X)


def RCP(out, in_):
    return lambda e: e.reciprocal(out, in_)


def MSET(ap, v):
    return lambda e: e.memset(ap, v)


def DMA(out, in_):
    return lambda e: e.dma_start(out=out, in_=in_)


def build(depth=DEPTH, debug=False, stop=None):
    nc = bass.Bass("TRN2", target_bir_lowering=False, num_devices=8)

    def din(name, shape, dt=F32):
        return nc.dram_tensor(name, list(shape), dt, kind="ExternalInput").ap()

    xT = din("xT", [D, TOK])
    w_in = din("w_in", [depth, D, 8200])
    w_gate = din("w_gate", [depth, D, 6144])
    w_br = [din(f"w_br{i}", [depth, 1024, D]) for i in range(3)]
    w_out = din("w_out", [depth, D, D])
    w_up = din("w_up", [depth, D, 8192])
    w_down = din("w_down", [depth, 8192, D])
    pvec = din("pvec", [128, DEPTH * NPV])
    cst = din("cst", [128, 512])
    dmask_d = din("dmask", [128, 512])
    rope_d = din("rope", [128, 2048])
    sel_d = din("sel", [128, 8])
    b_forget = din("b_forget", [DEPTH, 8])
    gln_g = din("gln_g", [DEPTH, 1024])
    gln_b = din("gln_b", [DEPTH, 1024])
    gws = din("gws", [DEPTH, 4, 128, 128])
    gbs = din("gbs", [DEPTH, 4, 128])
    lamv_d = din("lamv", [DEPTH, 512])
    dng = din("dng", [DEPTH, 256])
    outT = nc.dram_tensor("outT", [D, TOK], F32, kind="ExternalOutput").ap()
    dbg = None
    if debug:
        dbg = nc.dram_tensor("dbg", [3072, TOK], F32, kind="ExternalOutput").ap()

    kvloc = nc.dram_tensor("kvloc", [4096, 1024], BF16).ap()
    kvall = nc.dram_tensor("kvall", [4 * 4096, 1024], BF16).ap()
    lfloc = nc.dram_tensor("lfloc", [1024, 8], F32).ap()
    lfall = nc.dram_tensor("lfall", [4096, 8], F32).ap()

    with ExitStack() as st:
        ec = st.enter_context
        xhi = ec(nc.sbuf_tensor("xhi", [128, KC, TOK], BF16))
        xlo = ec(nc.sbuf_tensor("xlo", [128, KC, TOK], BF16))
        ring = [ec(nc.sbuf_tensor(f"ring{i}", [128, 16, 256], BF16)) for i in range(3)]
        SCR = ec(nc.sbuf_tensor("scr", [128, 41024], BF16))
        TMP = ec(nc.sbuf_tensor("tmp", [128, 12288], BF16))
        c16 = ec(nc.sbuf_tensor("c16", [128, 3, 128], BF16))
        c32 = ec(nc.sbuf_tensor("c32", [128, 4, 128], F32))
        dmask = ec(nc.sbuf_tensor("dmsk", [128, 4, 128], BF16))
        pv = ec(nc.sbuf_tensor("pv", [128, DEPTH * NPV], F32))
        bfbc = ec(nc.sbuf_tensor("bfbc", [128, 8], F32))
        lamt = ec(nc.sbuf_tensor("lamt", [128, 512], F32))
        lamr = ec(nc.sbuf_tensor("lamr", [128, 8], F32))
        dngbc = ec(nc.sbuf_tensor("dngbc", [128, 256], F32))
        WmT = ec(nc.sbuf_tensor("WmT", [128, 4, 128], BF16))
        Cc = ec(nc.sbuf_tensor("Cc", [128, 32, 8], F32))
        Ee = ec(nc.sbuf_tensor("Ee", [128, 33, 8], F32))
        small = ec(nc.sbuf_tensor("small", [128, 64], F32))
        selt = ec(nc.sbuf_tensor("selt", [128, 8], F32))
        refE = ec(nc.sbuf_tensor("refE", [128, 8, 8], F32))
        psb = [ec(nc.psum_tensor(f"ps{i}", [128, 512], F32)) for i in range(7)]
        pst = ec(nc.psum_tensor("pst", [128, 1024], BF16))

        S = Sched(nc)
        ident16, maskcp16, ones16 = c16[:, 0, :], c16[:, 1, :], c16[:, 2, :]
        ident32, tri32, ones32, RmT32 = c32[:, 0, :], c32[:, 1, :], c32[:, 2, :], c32[:, 3, :]

        def scr(b0, b1, dt=BF16):
            v = SCR[:, b0 // 2:b1 // 2]
            return v if dt == BF16 else v.bitcast(F32)

        def tmp(b0, b1, dt=BF16):
            v = TMP[:, b0 // 2:b1 // 2]
            return v if dt == BF16 else v.bitcast(F32)

        ps_bufs = [Buf(f"ps{i}") for i in range(7)]
        pst_buf = Buf("pst")
        ps_state = {"i": 0}

        def PS():
            i = ps_state["i"]
            ps_state["i"] = (i + 1) % 7
            return psb[i], ps_bufs[i]

        ring_bufs = [Buf(f"ring{i}") for i in range(3)]
        ring_state = {"i": 0}

        def load_panel(w2d, nkc):
            i = ring_state["i"]
            ring_state["i"] = (i + 1) % 3
            ncols = w2d.shape[1]
            dst = ring[i][:, 0:nkc, 0:ncols]
            S.op("pool", DMA(dst, w2d.rearrange("(c p) n -> p c n", p=128)), writes=[ring_bufs[i]], dma_key=f"w{i}")
            return ring[i], ring_bufs[i]

        xbuf = Buf("x")

        def proj_fm(w2d, nkc, rhs, evac, th_list=(0, 1)):
            ncols = w2d.shape[1]
            for p in range(ncols // 256):
                slot, sb = load_panel(w2d[:, p * 256:(p + 1) * 256], nkc)
                for j in range(2):
                    for th in th_list:
                        ps, pb = PS()
                        for kc in range(nkc):
                            r_ap, r_buf = rhs(kc, th)
                            S.op("pe", MM(ps[:, :], slot[:, kc, j * 128:(j + 1) * 128], r_ap, kc == 0, kc == nkc - 1),
                                 reads=[sb, r_buf], writes=[pb])
                        evac(p * 2 + j, th, ps, pb)

        def x_rhs(kc, th):
            return xhi[:, kc, th * 512:(th + 1) * 512], xbuf

        cbuf = Buf("const")
        S.op("pool", DMA(c16[:, :, :], cst[:, 0:384].rearrange("p (a b) -> p a b", a=3)), writes=[cbuf], dma_key="c")
        S.op("sp", DMA(c32[:, :, :], cst[:, 0:512].rearrange("p (a b) -> p a b", a=4)), writes=[cbuf], dma_key="c")
        S.op("pool", DMA(dmask[:, :, :], dmask_d.rearrange("p (a b) -> p a b", a=4)), writes=[cbuf], dma_key="c")
        S.op("sp", DMA(pv[:, :], pvec), writes=[cbuf], dma_key="c")
        S.op("sp", DMA(selt[:, :], sel_d), writes=[cbuf], dma_key="c")

        xin_t = [tmp(0, 2048, F32), tmp(2048, 4096, F32)]
        xin_b = [Buf("xin0"), Buf("xin1")]
        n = 0
        for c in range(KC):
            for th in range(2):
                t, tb = xin_t[n % 2], xin_b[n % 2]
                S.op("sp", DMA(t, xT[c * 128:(c + 1) * 128, th * 512:(th + 1) * 512]), writes=[tb], dma_key=f"xin{n % 2}")
                S.op("act", ACT(xhi[:, c, th * 512:(th + 1) * 512], t, AF.Copy), reads=[tb], writes=[xbuf])
                S.op("dve", TT(xlo[:, c, th * 512:(th + 1) * 512], t, xhi[:, c, th * 512:(th + 1) * 512], ALU.subtract),
                     reads=[tb, xbuf], writes=[xbuf])
                n += 1

        kvloc_b, kvall_b, lfloc_b, lfall_b = [Buf(f"kvloc{i}") for i in range(8)], Buf("kvall"), Buf("lfloc"), Buf("lfall")
        outb = Buf("out")

        class _Stop(Exception):
            pass

        def chk(tag):
            if stop == tag:
                raise _Stop()

        try:
            chk("const")
            for L in range(depth):
                lam_init = 0.8 - 0.6 * math.exp(-0.3 * L)
                pvo = L * NPV
                S.barrier_all()
                lpb = Buf("lparams")
                S.op("sp", DMA(bfbc[:, :].unsqueeze(1), b_forget[L:L + 1, :].partition_broadcast(128)), writes=[lpb], dma_key="lp")
                S.op("sp", DMA(lamt[:, :].unsqueeze(1), lamv_d[L:L + 1, :].partition_broadcast(128)), writes=[lpb], dma_key="lp")
                S.op("sp", DMA(dngbc[:, :].unsqueeze(1), dng[L:L + 1, :].partition_broadcast(128)), writes=[lpb], dma_key="lp")
                S.op("dve", TT(lamt[:, 0:128], lamt[:, 0:128], lamt[:, 128:256], ALU.mult), reads=[lpb], writes=[lpb])
                S.op("dve", TT(lamt[:, 256:384], lamt[:, 256:384], lamt[:, 384:512], ALU.mult), reads=[lpb], writes=[lpb])
                S.op("dve", RSUM(lamr[:, 0:1], lamt[:, 0:128]), reads=[lpb], writes=[lpb])
                S.op("dve", RSUM(lamr[:, 1:2], lamt[:, 256:384]), reads=[lpb], writes=[lpb])
                S.op("act", ACT(lamr[:, 2:4], lamr[:, 0:2], AF.Exp), reads=[lpb], writes=[lpb])
                S.op("dve", TT(lamr[:, 4:5], lamr[:, 3:4], lamr[:, 2:3], ALU.subtract), reads=[lpb], writes=[lpb])
                S.op("dve", TS(lamr[:, 4:5], lamr[:, 4:5], -lam_init, None, ALU.add), reads=[lpb], writes=[lpb])
                S.op("dve", TS(dngbc[:, :], dngbc[:, :], 1.0 - lam_init, None, ALU.mult), reads=[lpb], writes=[lpb])
                neg_lam = lamr[:, 4:5]

                cosT = scr(32768, 36864, F32)
                sinT = scr(36864, 40960, F32)
                ropeb = Buf("rope")
                S.op("sp", DMA(cosT, rope_d[:, 0:1024]), writes=[ropeb], dma_key="lp")
                S.op("sp", DMA(sinT, rope_d[:, 1024:2048]), writes=[ropeb], dma_key="lp")

                rt32 = [tmp(0, 2048, F32), tmp(2048, 4096, F32)]
                rta = [tmp(4096, 6144, F32), tmp(6144, 8192, F32)]
                rtb = [Buf("rt0"), Buf("rt1")]
                rstate = {"i": 0}

                def rope_evac(dst_ap, dst_buf, th, ps, pb):
                    i = rstate["i"]
                    rstate["i"] = (i + 1) % 2
                    t32, ta, tb_ = rt32[i], rta[i], rtb[i]
                    S.op("act", ACT(t32, ps[:, :], AF.Copy), reads=[pb], writes=[tb_])
                    ps2, pb2 = PS()
                    S.op("pe", MM(ps2[:, :], RmT32, t32, True, True), reads=[tb_, cbuf], writes=[pb2])
                    S.op("dve", TT(ta, t32, cosT[:, th * 512:(th + 1) * 512], ALU.mult), reads=[tb_, ropeb], writes=[tb_])
                    S.op("dve", TT(t32, ps2[:, :], sinT[:, th * 512:(th + 1) * 512], ALU.mult), reads=[pb2, ropeb, tb_], writes=[tb_])
                    S.op("dve", TT(dst_ap, ta, t32, ALU.add), reads=[tb_], writes=[dst_buf])

                chk("lp")
                fq = scr(0, 16384).rearrange("p (h t) -> p h t", h=8)
                dq = scr(16384, 32768).rearrange("p (h t) -> p h t", h=8)
                ua = scr(32768, 49152).rearrange("p (h t) -> p h t", h=8)
                fqb = [[Buf(f"fq{h}_{t}") for t in range(2)] for h in range(8)]
                dqb = [[Buf(f"dq{h}_{t}") for t in range(2)] for h in range(8)]
                uab = [Buf(f"ua{c}") for c in range(8)]
                vpan = scr(49152, 81920).rearrange("p (a c n) -> p a c n", a=4, c=16)
                vpb = Buf("vpan")
                pending_chunks = []
                kst = [scr(0, 4096).rearrange("p (a t) -> p a t", a=2), scr(4096, 8192).rearrange("p (a t) -> p a t", a=2)]
                kstb = [Buf("kst0"), Buf("kst1")]
                vst = [scr(8192, 12288).rearrange("p (a t) -> p a t", a=8), scr(12288, 16384).rearrange("p (a t) -> p a t", a=8)]
                vstb = [Buf("vst0"), Buf("vst1")]

                def gather_chunk(ci):
                    S.op("pool", (lambda ci: lambda e: e.collective_compute(
                        "AllGather", ALU.bypass, replica_groups=[[0, 1, 2, 3], [4, 5, 6, 7]],
                        ins=[kvloc[ci * 512:(ci + 1) * 512, :]], outs=[kvall[ci * 2048:(ci + 1) * 2048, :]]))(ci),
                        reads=[kvloc_b[ci]], writes=[kvall_b], dma_key="cc", dma_inc=1)

                for sec, (col0, roped) in enumerate(((OFF_FK, False), (OFF_DK, True))):
                    def evac_k(oc, th, ps, pb, sec=sec, roped=roped):
                        pn, j = divmod(oc, 2)
                        kb_, ks = kstb[pn % 2], kst[pn % 2]
                        dst = ks[:, j, th * 512:(th + 1) * 512]
                        if roped:
                            rope_evac(dst, kb_, th, ps, pb)
                        else:
                            S.op("act", ACT(dst, ps[:, :], AF.Copy), reads=[pb], writes=[kb_])
                        if j == 1 and th == 1:
                            r0 = sec * 1024 + pn * 256
                            S.op("sp", DMA(kvloc[r0:r0 + 256, :].rearrange("(j p) t -> p j t", p=128), ks[:, :, :]),
                                 reads=[kb_], writes=[kvloc_b[r0 // 512]], dma_key=f"kst{pn % 2}")
                            if pn % 2 == 1:
                                pending_chunks.append(r0 // 512)
                    proj_fm(w_in[L, :, col0:col0 + 1024], KC, x_rhs, evac_k)

                def evac_dq(oc, th, ps, pb):
                    rope_evac(dq[:, oc, th * 512:(th + 1) * 512], dqb[oc][th], th, ps, pb)
                proj_fm(w_in[L, :, OFF_DQ:OFF_DQ + 1024], KC, x_rhs, evac_dq)

                for sec, col0 in enumerate((OFF_FV, OFF_DV)):
                    for p in range(4):
                        slot, sb = load_panel(w_in[L, :, col0 + p * 256:col0 + (p + 1) * 256], KC)
                        vs, vb_ = vst[p % 2], vstb[p % 2]
                        for tb in range(8):
                            ps, pb = PS()
                            for kc in range(KC):
                                S.op("pe", MM(ps[:, 0:256], xhi[:, kc, tb * 128:(tb + 1) * 128], slot[:, kc, :], kc == 0, kc == KC - 1),
                                     reads=[sb, xbuf], writes=[pb])
                            S.op("act", ACT(vs[:, tb, :], ps[:, 0:256], AF.Copy), reads=[pb], writes=[vb_])
                        r0 = (2 + sec) * 1024
                        S.op("sp", DMA(kvloc[r0:r0 + 1024, p * 256:(p + 1) * 256].rearrange("(tb p) n -> p tb n", p=128), vs[:, :, :]),
                             reads=[vb_], writes=[kvloc_b[r0 // 512], kvloc_b[r0 // 512 + 1]], dma_key=f"vst{p % 2}")
                    pending_chunks.append(r0 // 512)
                    pending_chunks.append(r0 // 512 + 1)

                slot, sb = load_panel(w_in[L, :, OFF_FF:OFF_FF + 8], KC)
                lfst = small[:, 0:64].rearrange("p (a h) -> p a h", a=8)
                lfb = Buf("lfst")
                for tb in range(8):
                    ps, pb = PS()
                    for kc in range(KC):
                        S.op("pe", MM(ps[:, 0:8], xhi[:, kc, tb * 128:(tb + 1) * 128], slot[:, kc, 0:8], kc == 0, kc == KC - 1),
                             reads=[sb, xbuf], writes=[pb])
                    S.op("dve", TT(lfst[:, tb, :], ps[:, 0:8], bfbc[:, :], ALU.add), reads=[pb, lpb], writes=[lfb])
                S.op("act", ACT(small[:, 0:64], small[:, 0:64], AF.Exp, scale=-1.0), reads=[lfb], writes=[lfb])
                S.op("act", ACT(small[:, 0:64], small[:, 0:64], AF.Ln, bias=1.0), reads=[lfb], writes=[lfb])
                S.op("sp", DMA(lfloc.rearrange("(tb p) h -> p tb h", p=128), lfst), reads=[lfb], writes=[lfloc_b], dma_key="lfst")

                chk("p1")
                for p in range(4):
                    S.op("pool", DMA(vpan[:, p, :, :], w_in[L, :, OFF_V + p * 256:OFF_V + (p + 1) * 256].rearrange("(c p) n -> p c n", p=128)),
                         writes=[vpb], dma_key="vpan")
                for ci in pending_chunks:
                    gather_chunk(ci)
                S.op("pool", lambda e: e.collective_compute("AllGather", ALU.bypass, replica_groups=[[0, 1, 2, 3], [4, 5, 6, 7]],
                                                            ins=[lfloc], outs=[lfall]),
                     reads=[lfloc_b], writes=[lfall_b], dma_key="cc", dma_inc=1)

                S.barrier_all(skip=("cc",))
                chk("cc")

                chk("g1")
                glnG = tmp(0, 4096, F32)
                glnB = tmp(4096, 8192, F32)
                v32 = tmp(8192, 12288, F32)
                vsq = tmp(12288, 14336, F32)
                vn = [tmp(14336, 16384), tmp(16384, 18432)]
                wraw = tmp(18432, 20480, F32).rearrange("p (g s) -> p g s", g=4)
                bsbc = tmp(20480, 24576, F32).rearrange("p (c t) -> p c t", c=8)
                gpb = Buf("gparams")
                S.op("sp", DMA(glnG.unsqueeze(1), gln_g[L:L + 1, :].partition_broadcast(128)), writes=[gpb], dma_key="lp")
                S.op("sp", DMA(glnB.unsqueeze(1), gln_b[L:L + 1, :].partition_broadcast(128)), writes=[gpb], dma_key="lp")
                for g in range(4):
                    for rep in range(2):
                        S.op("sp", DMA(bsbc[:, 2 * g + rep, :].unsqueeze(1), gbs[L, g:g + 1, :].partition_broadcast(128)), writes=[gpb], dma_key="lp")
                S.op("sp", DMA(wraw, gws[L].rearrange("g t s -> t g s")), writes=[gpb], dma_key="lp")
                wmb = Buf("wm")
                for g in range(4):
                    ps, pb = PS()
                    S.op("pe", MM(ps[:, 0:128], wraw[:, g, :], ident32, True, True), reads=[gpb, cbuf], writes=[pb])
                    S.op("dve", TT(WmT[:, g, :], ps[:, 0:128], tri32, ALU.mult), reads=[pb, cbuf], writes=[wmb])

                chk("g2")
                chk("g3")
                v32b, vsqb, stb = Buf("v32"), Buf("vsq"), Buf("gstat")
                vnb = [Buf("vn0"), Buf("vn1")]
                gs = small[:, 0:16]
                for tb in range(8):
                    for half in range(2):
                        ps, pb = PS()
                        for pp in range(2):
                            p = half * 2 + pp
                            for kc in range(KC):
                                S.op("pe", MM(ps[:, pp * 256:(pp + 1) * 256], xhi[:, kc, tb * 128:(tb + 1) * 128], vpan[:, p, kc, :], kc == 0, kc == KC - 1),
                                     reads=[vpb, xbuf], writes=[pb])
                        S.op("act", ACT(v32[:, half * 512:(half + 1) * 512], ps[:, :], AF.Gelu), reads=[pb], writes=[v32b])
                    if tb == 1:
                        chk("g7")
                    if tb == 0:
                        chk("g4")
                    S.op("dve", RSUM(gs[:, 0:1], v32), reads=[v32b], writes=[stb])
                    for half in range(2):
                        S.op("dve", TT(vsq, v32[:, half * 512:(half + 1) * 512], v32[:, half * 512:(half + 1) * 512], ALU.mult), reads=[v32b], writes=[vsqb])
                        S.op("dve", RSUM(gs[:, 1 + half:2 + half], vsq), reads=[vsqb], writes=[stb])
                    S.op("dve", TT(gs[:, 3:4], gs[:, 1:2], gs[:, 2:3], ALU.add), reads=[stb], writes=[stb])
                    S.op("dve", TS(gs[:, 4:5], gs[:, 0:1], 1.0 / 1024, None, ALU.mult), reads=[stb], writes=[stb])
                    S.op("dve", TT(gs[:, 5:6], gs[:, 4:5], gs[:, 4:5], ALU.mult), reads=[stb], writes=[stb])
                    S.op("dve", STT(gs[:, 6:7], gs[:, 3:4], 1.0 / 1024, gs[:, 5:6], ALU.mult, ALU.subtract), reads=[stb], writes=[stb])
                    S.op("dve", TS(gs[:, 7:8], gs[:, 6:7], 1e-5, None, ALU.add), reads=[stb], writes=[stb])
                    S.op("act", ACT(gs[:, 7:8], gs[:, 7:8], AF.Ln), reads=[stb], writes=[stb])
                    S.op("act", ACT(gs[:, 7:8], gs[:, 7:8], AF.Exp, scale=-0.5), reads=[stb], writes=[stb])
                    S.op("dve", TS(v32, v32, gs[:, 4:5], gs[:, 7:8], ALU.subtract, ALU.mult), reads=[stb, v32b], writes=[v32b])
                    S.op("dve", TT(v32, v32, glnG, ALU.mult), reads=[gpb, v32b], writes=[v32b])
                    if tb == 0:
                        chk("g5")
                    vnt, vntb = vn[tb % 2], vnb[tb % 2]
                    S.op("dve", TT(vnt, v32, glnB, ALU.add), reads=[gpb, v32b], writes=[vntb])
                    for bank in range(2):
                        ps, pb = PS()
                        for cc4 in range(4):
                            cc = bank * 4 + cc4
                            S.op("pe", MM(ps[:, cc4 * 128:(cc4 + 1) * 128], vnt[:, cc * 128:(cc + 1) * 128], WmT[:, cc // 2, :], True, True),
                                 reads=[vntb, wmb], writes=[pb])
                        if tb == 0 and bank == 0:
                            chk("g6")
                        for cc4 in range(4):
                            cc = bank * 4 + cc4
                            S.op("dve", TT(ua[:, cc, tb * 128:(tb + 1) * 128], ps[:, cc4 * 128:(cc4 + 1) * 128], bsbc[:, cc, :], ALU.add),
                                 reads=[pb, gpb], writes=[uab[cc]])

                S.barrier_all(skip=("cc",))

                def evac_fq(oc, th, ps, pb):
                    S.op("act", ACT(fq[:, oc, th * 512:(th + 1) * 512], ps[:, :], AF.Copy), reads=[pb], writes=[fqb[oc][th]])
                proj_fm(w_in[L, :, OFF_FQ:OFF_FQ + 1024], KC, x_rhs, evac_fq)

                ut = [tmp(0, 2048, F32), tmp(2048, 4096, F32)]
                utb = [Buf("ut0"), Buf("ut1")]
                ustate = {"i": 0}

                def evac_u(oc, th, ps, pb):
                    i = ustate["i"]
                    ustate["i"] = (i + 1) % 2
                    S.op("act", ACT(ut[i], ps[:, :], AF.Gelu), reads=[pb], writes=[utb[i]])
                    S.op("dve", TT(ua[:, oc, th * 512:(th + 1) * 512], ut[i], ua[:, oc, th * 512:(th + 1) * 512], ALU.mult),
                         reads=[utb[i], uab[oc]], writes=[uab[oc]])
                proj_fm(w_in[L, :, OFF_U:OFF_U + 1024], KC, x_rhs, evac_u)

                S.barrier_all()
                chk("p2b")
                KT = scr(49152, 65536).rearrange("p (m r t) -> p m r t", m=2, r=4)
                Vb = scr(65536, 82048).rearrange("p (k c) -> p k c", c=258)
                ktb = [Buf("kt0"), Buf("kt1")]
                vbb = Buf("vb")
                PT = [tmp(0, 1024), tmp(1024, 2048), tmp(2048, 3072)]
                ptb = [Buf("pt0"), Buf("pt1"), Buf("pt2")]
                otmp = tmp(3072, 7168, F32).rearrange("p (j c) -> p j c", j=4)
                ores = tmp(7168, 11264, F32).rearrange("p (j c) -> p j c", j=4)
                obf = tmp(11264, 13312).rearrange("p (j c) -> p j c", j=4)
                biasT = tmp(13312, 21504, F32).rearrange("p (k m h) -> p k m h", k=32, m=8)
                lftm = tmp(21504, 22528, F32)
                otb, orb, obb, bib, lfb2 = Buf("otmp"), Buf("ores"), Buf("obf"), Buf("bias"), Buf("lftm")
                osm = small[:, 16:48]
                osb = Buf("osm")

                lftm4 = lftm.rearrange("p (g i h) -> p g i h", g=8, i=4)
                for i_ in range(4):
                    S.op("sp", DMA(lftm4[:, :, i_, :], lfall[i_ * 1024:(i_ + 1) * 1024, :].rearrange("(g p) h -> p g h", p=128)),
                         reads=[lfall_b], writes=[lfb2], dma_key="lp")
                ps, pb = PS()
                S.op("pe", MM(ps[:, 0:256], tri32, lftm, True, True), reads=[lfb2, cbuf], writes=[pb])
                ps2, pb2 = PS()
                S.op("pe", MM(ps2[:, 0:256], ones32, lftm, True, True), reads=[lfb2, cbuf], writes=[pb2])
                ceb = Buf("ce")
                S.op("dve", MSET(Ee[:, 0, :], 0.0), writes=[ceb])
                for kb in range(32):
                    S.op("dve", TT(Ee[:, kb + 1, :], Ee[:, kb, :], ps2[:, kb * 8:(kb + 1) * 8], ALU.add), reads=[pb2, ceb], writes=[ceb])
                S.op("dve", TT(Cc[:, :, :], ps[:, 0:256].rearrange("p (k h) -> p k h", k=32), Ee[:, 0:32, :], ALU.add), reads=[pb, ceb], writes=[ceb])
                Ee4 = Ee[:, 0:32, :].rearrange("p (m i) h -> p m i h", i=4)
                S.op("dve", TS(refE[:, :, :], Ee4[:, :, 0, :], selt[:, 0:1], None, ALU.mult), reads=[ceb, cbuf], writes=[bib])
                for i_ in range(1, 4):
                    S.op("dve", STT(refE[:, :, :], Ee4[:, :, i_, :], selt[:, i_:i_ + 1], refE[:, :, :], ALU.mult, ALU.add), reads=[ceb, cbuf, bib], writes=[bib])
                for m in range(8):
                    S.op("dve", TT(biasT[:, :, m, :], Cc[:, :, :], refE[:, m:m + 1, :].broadcast_to([128, 32, 8]), ALU.subtract),
                         reads=[ceb, bib], writes=[bib])
                    S.op("dve", TT(biasT[:, 4 * m:4 * m + 4, m, :], biasT[:, 4 * m:4 * m + 4, m, :],
                                   selt[:, 4:8].unsqueeze(2).broadcast_to([128, 4, 8]), ALU.add), reads=[bib, cbuf], writes=[bib])
                chk("a1")
                pt_state = {"i": 0, "o": 0, "s": 0}

                def attn_unit(kind, u):
                    if kind == "fox":
                        krow0, vrow0, vcol0 = 0, 2048, u * 256
                        qt, qtb = fq, fqb
                    else:
                        krow0, vrow0, vcol0 = 1024, 3072, u * 256
                        qt, qtb = dq, dqb
                    for mp in range(2):
                        hm = 2 * u + mp
                        ci = krow0 // 512 + hm // 4
                        src = kvall[ci * 2048:(ci + 1) * 2048, :].rearrange("(r q) t -> q r t", r=4)[(hm % 4) * 128:(hm % 4 + 1) * 128, :, :]
                        S.op("sp", DMA(KT[:, mp, :, :], src), reads=[kvall_b], writes=[ktb[mp]], dma_key=f"kt{mp}")
                    for r_ in range(4):
                        for half in range(2):
                            ci = vrow0 // 512 + half
                            rows = kvall[ci * 2048 + r_ * 512:ci * 2048 + (r_ + 1) * 512, :]
                            k0 = r_ * 8 + half * 4
                            if kind == "fox":
                                for hh in range(2):
                                    S.op("sp", DMA(Vb[:, k0:k0 + 4, hh * 129:hh * 129 + 128],
                                                   rows[:, vcol0 + hh * 128:vcol0 + (hh + 1) * 128].rearrange("(g p) n -> p g n", p=128)),
                                         reads=[kvall_b], writes=[vbb], dma_key="vb")
                            else:
                                S.op("sp", DMA(Vb[:, k0:k0 + 4, 0:256], rows[:, vcol0:vcol0 + 256].rearrange("(g p) n -> p g n", p=128)),
                                     reads=[kvall_b], writes=[vbb], dma_key="vb")
                    if kind == "fox":
                        S.op("pool", MSET(Vb[:, :, 128:129], 1.0), writes=[vbb])
                        S.op("pool", MSET(Vb[:, :, 257:258], 1.0), writes=[vbb])
                    else:
                        S.op("pool", MSET(Vb[:, :, 256:257], 1.0), writes=[vbb])

                    for m0 in (0, 4):
                        tq = m0 // 4
                        for mp in range(2):
                            hm = 2 * u + mp
                            dvw = 129 if kind == "fox" else 257
                            vc0 = mp * 129 if kind == "fox" else 0
                            if kind == "fox":
                                O = [(psb[b_][:, 0:129], ps_bufs[b_]) for b_ in range(4)]
                            else:
                                O = [(psb[b_][:, 0:257], ps_bufs[b_]) for b_ in range(4)]
                            first = [True] * 4
                            ngroups = m0 + 4
                            def stage1(g, i):
                                j0 = max(0, g - m0)
                                c0 = j0 * 128
                                kbi = 4 * g + i
                                sb_ = 4 + pt_state["s"] % 3
                                pt_state["s"] += 1
                                ps, pb = psb[sb_], ps_bufs[sb_]
                                S.op("pe", MM(ps[:, c0:512], KT[:, mp, i, g * 128:(g + 1) * 128], qt[:, hm, m0 * 128 + c0:(m0 + 4) * 128], True, True),
                                     reads=[ktb[mp], qtb[hm][tq]], writes=[pb])
                                pi = pt_state["i"]
                                pt_state["i"] = (pi + 1) % 3
                                pt, ptbuf = PT[pi], ptb[pi]
                                if kind == "fox":
                                    for j in range(j0, 4):
                                        S.op("act", ACT(pt[:, j * 128:(j + 1) * 128], ps[:, j * 128:(j + 1) * 128], AF.Exp,
                                                        bias=biasT[:, kbi, m0 + j, hm:hm + 1], scale=SCALE),
                                             reads=[pb, bib], writes=[ptbuf])
                                else:
                                    S.op("act", ACT(pt[:, c0:512], ps[:, c0:512], AF.Exp, scale=SCALE), reads=[pb], writes=[ptbuf])
                                if g >= m0:
                                    S.op("pool", TT(pt[:, c0:c0 + 128], pt[:, c0:c0 + 128], dmask[:, i, :], ALU.mult), reads=[ptbuf, cbuf], writes=[ptbuf])
                                return (g, i, j0, kbi, pt, ptbuf)

                            def stage2(tup):
                                g, i, j0, kbi, pt, ptbuf = tup
                                for j in range(j0, 4):
                                    is_last = (4 * (m0 + j) + 3 == kbi)
                                    S.op("pe", MM(O[j][0], pt[:, j * 128:(j + 1) * 128], Vb[:, i * 8 + g, vc0:vc0 + dvw], first[j], is_last),
                                         reads=[ptbuf, vbb], writes=[O[j][1]])
                                    first[j] = False

                            pend = []
                            for g in range(ngroups):
                                for i in range(4):
                                    pend.append(stage1(g, i))
                                    if len(pend) > 2:
                                        stage2(pend.pop(0))
                            while pend:
                                stage2(pend.pop(0))
                            for j in range(4):
                                oj, ojb = O[j]
                                tok0 = (m0 + j) * 128
                                if kind == "fox":
                                    S.op("dve", RCP(osm[:, j:j + 1], oj[:, 128:129]), reads=[ojb], writes=[osb])
                                    S.op("dve", TS(obf[:, j, 0:128], oj[:, 0:128], osm[:, j:j + 1], None, ALU.mult), reads=[ojb, osb], writes=[obb])
                                    S.op("pe", TR(pst[:, j * 128:(j + 1) * 128], obf[:, j, 0:128], ident16), reads=[obb, cbuf], writes=[pst_buf])
                                    S.op("act", ACT(fq[:, hm, tok0:tok0 + 128], pst[:, j * 128:(j + 1) * 128], AF.Copy), reads=[pst_buf], writes=[fqb[hm][tq]])
                                elif mp == 0:
                                    S.op("dve", RCP(osm[:, j:j + 1], oj[:, 256:257]), reads=[ojb], writes=[osb])
                                    S.op("dve", TS(otmp[:, j, :], oj[:, 0:256], osm[:, j:j + 1], None, ALU.mult), reads=[ojb, osb], writes=[otb])
                                else:
                                    S.op("dve", RCP(osm[:, 8 + j:9 + j], oj[:, 256:257]), reads=[ojb], writes=[osb])
                                    S.op("dve", TT(osm[:, 8 + j:9 + j], osm[:, 8 + j:9 + j], neg_lam, ALU.mult), reads=[osb, lpb], writes=[osb])
                                    S.op("dve", STT(ores[:, j, :], oj[:, 0:256], osm[:, 8 + j:9 + j], otmp[:, j, :], ALU.mult, ALU.add),
                                         reads=[ojb, osb, otb], writes=[orb])
                                    S.op("dve", TT(otmp[:, j, :], ores[:, j, :], ores[:, j, :], ALU.mult), reads=[orb], writes=[otb])
                                    S.op("dve", RSUM(osm[:, 16 + j:17 + j], otmp[:, j, :]), reads=[otb], writes=[osb])
                                    S.op("dve", TS(osm[:, 16 + j:17 + j], osm[:, 16 + j:17 + j], 1.0 / 256, 1e-5, ALU.mult, ALU.add), reads=[osb], writes=[osb])
                                    S.op("act", ACT(osm[:, 16 + j:17 + j], osm[:, 16 + j:17 + j], AF.Ln), reads=[osb], writes=[osb])
                                    S.op("act", ACT(osm[:, 16 + j:17 + j], osm[:, 16 + j:17 + j], AF.Exp, scale=-0.5), reads=[osb], writes=[osb])
                                    S.op("dve", STT(obf[:, j, :], ores[:, j, :], osm[:, 16 + j:17 + j], dngbc[:, :], ALU.mult, ALU.mult),
                                         reads=[orb, osb, lpb], writes=[obb])
                                    for c in range(2):
                                        S.op("pe", TR(pst[:, (2 * j + c) * 128:(2 * j + c + 1) * 128], obf[:, j, c * 128:(c + 1) * 128], ident16),
                                             reads=[obb, cbuf], writes=[pst_buf])
                                    for c in range(2):
                                        S.op("act", ACT(dq[:, 2 * u + c, tok0:tok0 + 128], pst[:, (2 * j + c) * 128:(2 * j + c + 1) * 128], AF.Copy),
                                             reads=[pst_buf], writes=[dqb[2 * u + c][tq]])

                for u in range(4):
                    attn_unit("fox", u)
                    if u == 0:
                        chk("a2")
                chk("a3")
                for u in range(4):
                    attn_unit("diff", u)

                S.barrier_all()
                if debug:
                    dt_ = [tmp(0, 2048, F32), tmp(2048, 4096, F32)]
                    db_ = [Buf("dbg0"), Buf("dbg1")]
                    n_ = 0
                    for si_, (src_, bufs_) in enumerate(((ua, None), (fq, fqb), (dq, dqb))):
                        for c_ in range(8):
                            for th_ in range(2):
                                rb_ = uab[c_] if bufs_ is None else bufs_[c_][th_]
                                S.op("act", ACT(dt_[n_ % 2], src_[:, c_, th_ * 512:(th_ + 1) * 512], AF.Copy), reads=[rb_], writes=[db_[n_ % 2]])
                                S.op("sp", DMA(dbg[si_ * 1024 + c_ * 128:si_ * 1024 + (c_ + 1) * 128, th_ * 512:(th_ + 1) * 512], dt_[n_ % 2]),
                                     reads=[db_[n_ % 2]], writes=[outb], dma_key=f"dbg{n_ % 2}")
                                n_ += 1
                chk("p3")
                merged = scr(49152, 81920).rearrange("p (c t) -> p c t", c=16)
                mgb = [Buf(f"mg{c}") for c in range(16)]
                gsig = [tmp(0, 4096).rearrange("p (a t) -> p a t", a=2), tmp(4096, 8192).rearrange("p (a t) -> p a t", a=2)]
                gsb = [Buf("gs0"), Buf("gs1")]
                macc = tmp(8192, 16384, F32).rearrange("p (a t) -> p a t", a=2)
                mab = [[Buf(f"ma{a}{t}") for t in range(2)] for a in range(2)]
                tmpm = [tmp(16384, 18432, F32), tmp(18432, 20480, F32)]
                tmb = [Buf("tm0"), Buf("tm1")]
                tstate = {"i": 0}
                for ocp in range(8):
                    for bi in range(3):
                        gs_t, gs_b = gsig[bi % 2], gsb[bi % 2]

                        def evac_gate(oc, th, ps, pb, bi=bi, gs_t=gs_t, gs_b=gs_b, ocp=ocp):
                            col = pvo + bi * 16 + ocp * 2 + oc
                            S.op("act", ACT(gs_t[:, oc, th * 512:(th + 1) * 512], ps[:, :], AF.Sigmoid, bias=pv[:, col:col + 1]),
                                 reads=[pb, cbuf], writes=[gs_b])
                        proj_fm(w_gate[L, :, bi * 2048 + ocp * 256:bi * 2048 + (ocp + 1) * 256], KC, x_rhs, evac_gate)

                        src_t = (ua, fq, dq)[bi]

                        def br_rhs(kc, th, bi=bi, src_t=src_t):
                            if bi == 0:
                                b = uab[kc]
                            elif bi == 1:
                                b = fqb[kc][th]
                            else:
                                b = dqb[kc][th]
                            return src_t[:, kc, th * 512:(th + 1) * 512], b

                        def evac_br(oc, th, ps, pb, bi=bi, gs_t=gs_t, gs_b=gs_b, ocp=ocp):
                            g_ap = gs_t[:, oc, th * 512:(th + 1) * 512]
                            acc = macc[:, oc, th * 512:(th + 1) * 512]
                            ab = mab[oc][th]
                            if bi == 0:
                                S.op("dve", TT(acc, ps[:, :], g_ap, ALU.mult), reads=[pb, gs_b], writes=[ab])
                            else:
                                ti = tstate["i"]
                                tstate["i"] = (ti + 1) % 2
                                S.op("dve", TT(tmpm[ti], ps[:, :], g_ap, ALU.mult), reads=[pb, gs_b], writes=[tmb[ti]])
                                if bi == 1:
                                    S.op("dve", TT(acc, acc, tmpm[ti], ALU.add), reads=[tmb[ti], ab], writes=[ab])
                                else:
                                    S.op("dve", TT(merged[:, ocp * 2 + oc, th * 512:(th + 1) * 512], acc, tmpm[ti], ALU.add),
                                         reads=[tmb[ti], ab], writes=[mgb[ocp * 2 + oc]])
                        proj_fm(w_br[bi][L, :, ocp * 256:(ocp + 1) * 256], 8, br_rhs, evac_br)

                S.barrier_all()

                chk("p4")
                mean_t = tmp(0, 2048, F32)
                rstd_t = tmp(2048, 4096, F32)
                sqt = [tmp(4096, 6144, F32), tmp(6144, 8192, F32)]
                y32 = [tmp(8192, 10240, F32), tmp(10240, 12288, F32)]
                sqb = [Buf("sq0"), Buf("sq1")]
                y32b = [Buf("y0"), Buf("y1")]
                stat_b = Buf("lnstat")

                def layer_norm(pre, preb, th, gcol, bcol, final):
                    ps_s, pb_s = PS()
                    ps_q, pb_q = PS()
                    for c in range(KC):
                        S.op("pe", MM(ps_s[:, :], ones32, pre[:, c, :], c == 0, c == KC - 1), reads=[preb, cbuf], writes=[pb_s])
                    for c in range(KC):
                        S.op("act", ACT(sqt[c % 2], pre[:, c, :], AF.Square), reads=[preb], writes=[sqb[c % 2]])
                        S.op("pe", MM(ps_q[:, :], ones32, sqt[c % 2], c == 0, c == KC - 1), reads=[sqb[c % 2], cbuf], writes=[pb_q])
                    S.op("dve", TS(mean_t, ps_s[:, :], 1.0 / D, None, ALU.mult), reads=[pb_s], writes=[stat_b])
                    S.op("dve", TT(rstd_t, mean_t, mean_t, ALU.mult), reads=[stat_b], writes=[stat_b])
                    S.op("dve", STT(rstd_t, ps_q[:, :], 1.0 / D, rstd_t, ALU.mult, ALU.subtract), reads=[pb_q, stat_b], writes=[stat_b])
                    S.op("dve", TS(rstd_t, rstd_t, 1e-5, None, ALU.add), reads=[stat_b], writes=[stat_b])
                    S.op("act", ACT(rstd_t, rstd_t, AF.Ln), reads=[stat_b], writes=[stat_b])
                    S.op("act", ACT(rstd_t, rstd_t, AF.Exp, scale=-0.5), reads=[stat_b], writes=[stat_b])
                    for c in range(KC):
                        y, yb = y32[c % 2], y32b[c % 2]
                        S.op("dve", TT(y, pre[:, c, :], mean_t, ALU.subtract), reads=[preb, stat_b], writes=[yb])
                        S.op("dve", TT(y, y, rstd_t, ALU.mult), reads=[stat_b, yb], writes=[yb])
                        S.op("dve", TS(y, y, pv[:, gcol + c:gcol + c + 1], pv[:, bcol + c:bcol + c + 1], ALU.mult, ALU.add), reads=[cbuf, yb], writes=[yb])
                        if final:
                            S.op("sp", DMA(outT[c * 128:(c + 1) * 128, th * 512:(th + 1) * 512], y), reads=[yb], writes=[outb], dma_key=f"out{c % 2}")
                        else:
                            hi = xhi[:, c, th * 512:(th + 1) * 512]
                            S.op("act", ACT(hi, y, AF.Copy), reads=[yb], writes=[xbuf])
                            S.op("pool", TT(xlo[:, c, th * 512:(th + 1) * 512], y, hi, ALU.subtract), reads=[yb, xbuf], writes=[xbuf])

                preh = scr(0, 32768, F32).rearrange("p (c t) -> p c t", c=16)
                for th in range(2):
                    prehb = Buf(f"preh{th}")

                    def mg_rhs(kc, th_, th=th):
                        return merged[:, kc, th * 512:(th + 1) * 512], mgb[kc]

                    def evac_mix(oc, th_, ps, pb, th=th, prehb=prehb):
                        ti = tstate["i"]
                        tstate["i"] = (ti + 1) % 2
                        t_, tb_ = tmpm[ti], tmb[ti]
                        S.op("dve", TT(t_, xhi[:, oc, th * 512:(th + 1) * 512], xlo[:, oc, th * 512:(th + 1) * 512], ALU.add), reads=[xbuf], writes=[tb_])
                        S.op("dve", STT(preh[:, oc, :], t_, ALPHA, ps[:, :], ALU.mult, ALU.add), reads=[tb_, pb], writes=[prehb])
                    proj_fm(w_out[L, :, :], KC, mg_rhs, evac_mix, th_list=(th,))
                    layer_norm(preh, prehb, th, pvo + 48, pvo + 64, False)
                    S.barrier_all()

                chk("p5")
                pre = scr(0, 65536, F32).rearrange("p (c t) -> p c t", c=16)
                hT = scr(65536, 81920).rearrange("p (c t) -> p c t", c=8)
                preb = [Buf("pre0"), Buf("pre1")]
                hb = [Buf(f"h{c}") for c in range(8)]
                for q in range(8):
                    def evac_up(oc, th, ps, pb, q=q):
                        ti = tstate["i"]
                        tstate["i"] = (ti + 1) % 2
                        S.op("act", ACT(tmpm[ti], ps[:, :], AF.Relu), reads=[pb], writes=[tmb[ti]])
                        S.op("dve", TT(hT[:, oc, th * 512:(th + 1) * 512], tmpm[ti], tmpm[ti], ALU.mult), reads=[tmb[ti]], writes=[hb[oc]])
                    proj_fm(w_up[L, :, q * 1024:(q + 1) * 1024], KC, x_rhs, evac_up)

                    def h_rhs(kc, th):
                        return hT[:, kc, th * 512:(th + 1) * 512], hb[kc]

                    def evac_dn(oc, th, ps, pb, q=q):
                        dst = pre[:, oc, th * 512:(th + 1) * 512]
                        if q == 0:
                            ti = tstate["i"]
                            tstate["i"] = (ti + 1) % 2
                            t_, tb_ = tmpm[ti], tmb[ti]
                            S.op("dve", TT(t_, xhi[:, oc, th * 512:(th + 1) * 512], xlo[:, oc, th * 512:(th + 1) * 512], ALU.add), reads=[xbuf], writes=[tb_])
                            S.op("dve", STT(dst, t_, ALPHA, ps[:, :], ALU.mult, ALU.add), reads=[tb_, pb], writes=[preb[th]])
                        else:
                            S.op("dve", TT(dst, dst, ps[:, :], ALU.add), reads=[pb, preb[th]], writes=[preb[th]])
                    proj_fm(w_down[L, q * 1024:(q + 1) * 1024, :], 8, h_rhs, evac_dn)
                S.barrier_all()
                for th in range(2):
                    layer_norm(pre[:, :, th * 512:(th + 1) * 512], preb[th], th, pvo + 80, pvo + 96, L == depth - 1)

        except _Stop:
            pass
        S.barrier_all()
        S.emit(st)
        nops = sum(len(v) for v in S.ops.values())
        print(f"[kernel] ops={nops} sems={S.n_sems}", flush=True)
    return nc


def _consts():
    ident = np.eye(128, dtype=np.float32)
    p = np.arange(128)[:, None]
    c = np.arange(128)[None, :]
    tri = (c >= p).astype(np.float32)
    ones = np.ones((128, 128), np.float32)
    rmt = np.zeros((128, 128), np.float32)
    for d in range(64):
        rmt[d + 64, d] = -1.0
        rmt[d, d + 64] = 1.0
    return np.concatenate([ident, tri, ones, rmt], axis=1)


def _dmask(r):
    p = np.arange(128)[:, None]
    c = np.arange(128)[None, :]
    tri = (c >= p).astype(np.float32)
    out = np.zeros((128, 4, 128), np.float32)
    for i in range(4):
        if i < r:
            out[:, i, :] = 1.0
        elif i == r:
            out[:, i, :] = tri
    return out.reshape(128, 512)


def _rope(r):
    pos = np.concatenate([(r + 4 * m) * 128 + np.arange(128) for m in range(8)]).astype(np.float32)
    inv_freq = (10000.0 ** (-np.arange(0, 128, 2, dtype=np.float32) / 128)).astype(np.float32)
    ang = pos[None, :] * inv_freq[:, None]
    cos = np.cos(ang).astype(np.float32)
    sin = np.sin(ang).astype(np.float32)
    return np.concatenate([np.concatenate([cos, cos], 0), np.concatenate([sin, sin], 0)], axis=1)


_NC_CACHE = {}


def kernel(x, w_in, b_forget, gmlp_ln_g, gmlp_ln_b, gmlp_w_s, gmlp_b_s, lam_q1, lam_k1, lam_q2, lam_k2, diff_norm_g,
           w_branch_a, w_branch_b, w_branch_c, w_gate, b_gate, w_out, ln_mix_g, ln_mix_b, w_up, w_down, ln_mlp_g, ln_mlp_b,
           _depth=DEPTH, _debug=False, _stop=None):
    f = lambda a: np.ascontiguousarray(np.asarray(a, dtype=np.float32))
    x = f(x)
    key = (_depth, _debug, _stop)
    if key not in _NC_CACHE:
        _NC_CACHE[key] = build(_depth, _debug, _stop)
    nc = _NC_CACHE[key]

    def fm(v):
        v = f(v)
        return v.reshape(DEPTH, -1, 128).transpose(2, 0, 1)
    pvec = np.concatenate([fm(b_gate), fm(ln_mix_g), fm(ln_mix_b), fm(ln_mlp_g), fm(ln_mlp_b)], axis=2)
    pvec = np.ascontiguousarray(pvec.reshape(128, DEPTH * NPV))
    lamv = np.ascontiguousarray(np.concatenate([f(lam_q1), f(lam_k1), f(lam_q2), f(lam_k2)], axis=1))
    shared = {
        "w_in": f(w_in[:_depth]), "w_gate": f(w_gate[:_depth]), "w_br0": f(w_branch_a[:_depth]), "w_br1": f(w_branch_b[:_depth]),
        "w_br2": f(w_branch_c[:_depth]), "w_out": f(w_out[:_depth]), "w_up": f(w_up[:_depth]), "w_down": f(w_down[:_depth]), "pvec": pvec, "cst": _consts(),
        "b_forget": f(b_forget), "gln_g": f(gmlp_ln_g), "gln_b": f(gmlp_ln_b), "gws": f(gmlp_w_s), "gbs": f(gmlp_b_s),
        "lamv": lamv, "dng": f(diff_norm_g),
    }
    in_maps = []
    for c in range(8):
        b, r = divmod(c, 4)
        blocks = [x[b, (r + 4 * m) * 128:(r + 4 * m + 1) * 128, :] for m in range(8)]
        xs = np.concatenate(blocks, axis=0)
        m = dict(shared)
        m["xT"] = np.ascontiguousarray(xs.T)
        m["dmask"] = _dmask(r)
        m["rope"] = _rope(r)
        sel = np.zeros((128, 8), np.float32)
        sel[:, r] = 1.0
        sel[:, 4 + r + 1:8] = -30000.0
        m["sel"] = sel
        in_maps.append(m)
    res = run_bass_kernel_spmd(nc, in_maps, core_ids=list(range(8)))
    out = np.empty((2, 4096, 2048), np.float32)
    for c in range(8):
        b, r = divmod(c, 4)
        o = res.results[c]["outT"].T
        for m in range(8):
            out[b, (r + 4 * m) * 128:(r + 4 * m + 1) * 128, :] = o[m * 128:(m + 1) * 128, :]
    if _debug:
        return out, res
    return out
```
